# Optimizing a Trainium2 kernel written in Bass

```python
import jax, jax.numpy as jnp
from jax import lax
import numpy as np

D_MODEL = 1024
BATCH = 4
SEQ = 8192
DEPTH = 2

CHUNK = 64
EXPAND = 2
E_A = EXPAND * D_MODEL // 2
E_B = EXPAND * D_MODEL // 2
A_GROUPS = 8
A_BLOCK = 128
B_GROUPS = 8
CONV_WIDTH = 31
PLE_DIM = 256
NORM_EPS = 1e-6
LN_EPS = 1e-5
N_IN = 3 * E_A + 3 * E_B + 2 * D_MODEL
SPLIT_IDX = [E_A, 2 * E_A, 3 * E_A, 3 * E_A + E_B, 3 * E_A + 2 * E_B,
             3 * E_A + 3 * E_B, 3 * E_A + 3 * E_B + D_MODEL]

kernel_name = "hybrid_gmlp_conformer_gated_trunk"


def _rmsnorm(x, g):
    xf = x.astype(jnp.float32)
    y = xf * lax.rsqrt(jnp.mean(xf * xf, axis=-1, keepdims=True) + NORM_EPS)
    return (y * g.astype(jnp.float32)).astype(x.dtype)


def _normalize(x):
    xf = x.astype(jnp.float32)
    mu = jnp.mean(xf, axis=-1, keepdims=True)
    var = jnp.mean(jnp.square(xf - mu), axis=-1, keepdims=True)
    return (xf - mu) * lax.rsqrt(var + LN_EPS)


def _chunk_causal_mask():
    c = jnp.arange(A_BLOCK) // CHUNK
    return c[:, None] >= c[None, :]


def _spatial_gating_branch(u, v, z, ln_g, ln_b, ws, bs):
    b_, s_, _ = v.shape
    v = (_normalize(v) * ln_g.astype(jnp.float32) + ln_b.astype(jnp.float32)).astype(u.dtype)
    ws = jnp.where(_chunk_causal_mask()[None], ws, jnp.zeros_like(ws))
    vb = v.reshape(b_, s_ // A_BLOCK, A_BLOCK, A_GROUPS, E_A // A_GROUPS)
    mixed = jnp.einsum('gij,bnjgc->bnigc', ws, vb) + bs.T[:, :, None]
    mixed = mixed.reshape(b_, s_, E_A)
    return u * mixed * jax.nn.silu(z)


def _conformer_conv_branch(a, a_gate, z, conv_w, conv_b, gn_g, gn_b):
    h = a * jax.nn.sigmoid(a_gate)
    h = lax.conv_general_dilated(
        h, conv_w[:, None, :].astype(h.dtype), window_strides=(1,),
        padding=[(CONV_WIDTH - 1, 0)], dimension_numbers=('NWC', 'WIO', 'NWC'),
        feature_group_count=E_B) + conv_b
    b_, s_, _ = h.shape
    hn = _normalize(h.reshape(b_, s_, B_GROUPS, E_B // B_GROUPS)).reshape(b_, s_, E_B)
    hn = (hn * gn_g.astype(jnp.float32) + gn_b.astype(jnp.float32)).astype(a.dtype)
    return jax.nn.silu(hn) * jax.nn.silu(z)


def setup_inputs(seed: int = 0) -> dict:
    key = jax.random.key(seed)
    ks = jax.random.split(key, 20)
    f32 = jnp.float32
    nrm = lambda k, shape, scale: jax.random.normal(k, shape, f32) * scale
    return {
        "x": nrm(ks[0], (BATCH, SEQ, D_MODEL), 1.0),
        "p": nrm(ks[1], (DEPTH, BATCH, SEQ, PLE_DIM), 1.0),
        "norm_g": 1.0 + nrm(ks[2], (DEPTH, D_MODEL), 0.05),
        "w_in": nrm(ks[3], (DEPTH, D_MODEL, N_IN), D_MODEL ** -0.5),
        "b_in": nrm(ks[4], (DEPTH, N_IN), 0.02),
        "a_ln_g": 1.0 + nrm(ks[5], (DEPTH, E_A), 0.05),
        "a_ln_b": nrm(ks[6], (DEPTH, E_A), 0.02),
        "a_ws": nrm(ks[7], (DEPTH, A_GROUPS, A_BLOCK, A_BLOCK), A_BLOCK ** -0.5),
        "a_bs": 1.0 + nrm(ks[8], (DEPTH, A_GROUPS, A_BLOCK), 0.1),
        "b_conv_w": nrm(ks[9], (DEPTH, CONV_WIDTH, E_B), CONV_WIDTH ** -0.5),
        "b_conv_b": nrm(ks[10], (DEPTH, E_B), 0.02),
        "b_gn_g": 1.0 + nrm(ks[11], (DEPTH, E_B), 0.05),
        "b_gn_b": nrm(ks[12], (DEPTH, E_B), 0.02),
        "w_pa": nrm(ks[13], (DEPTH, E_A, D_MODEL), E_A ** -0.5),
        "w_pb": nrm(ks[14], (DEPTH, E_B, D_MODEL), E_B ** -0.5),
        "w_out": nrm(ks[15], (DEPTH, D_MODEL, D_MODEL), D_MODEL ** -0.5),
        "ple_norm_g": 1.0 + nrm(ks[16], (DEPTH, D_MODEL), 0.05),
        "w_ple_gate": nrm(ks[17], (DEPTH, D_MODEL, D_MODEL), D_MODEL ** -0.5),
        "w_ple": nrm(ks[18], (DEPTH, PLE_DIM, D_MODEL), PLE_DIM ** -0.5),
        "final_g": 1.0 + nrm(ks[19], (D_MODEL,), 0.05),
    }


def reference(x, p, norm_g, w_in, b_in, a_ln_g, a_ln_b, a_ws, a_bs, b_conv_w, b_conv_b,
              b_gn_g, b_gn_b, w_pa, w_pb, w_out, ple_norm_g, w_ple_gate, w_ple, final_g):
    for i in range(DEPTH):
        h = _rmsnorm(x, norm_g[i])
        proj = jnp.einsum('bsd,dn->bsn', h, w_in[i]) + b_in[i]
        u, v, za, ab, ab_gate, zb, g_a, g_b = jnp.split(proj, SPLIT_IDX, axis=-1)
        y_a = _spatial_gating_branch(jax.nn.gelu(u), jax.nn.gelu(v), za,
                                     a_ln_g[i], a_ln_b[i], a_ws[i], a_bs[i])
        y_b = _conformer_conv_branch(ab, ab_gate, zb, b_conv_w[i], b_conv_b[i],
                                     b_gn_g[i], b_gn_b[i])
        merged = (jax.nn.sigmoid(g_a) * jnp.einsum('bse,ed->bsd', y_a, w_pa[i])
                  + jax.nn.sigmoid(g_b) * jnp.einsum('bse,ed->bsd', y_b, w_pb[i]))
        x = x + jnp.einsum('bsd,de->bse', merged, w_out[i])
        ple_gate = jax.nn.sigmoid(jnp.einsum('bsd,de->bse', _rmsnorm(x, ple_norm_g[i]), w_ple_gate[i]))
        x = x + ple_gate * jnp.einsum('bsk,kd->bsd', p[i], w_ple[i])
    return _rmsnorm(x, final_g)
```

```python
import os
import numpy as np
from contextlib import ExitStack
import concourse.bass as bass
import concourse.mybir as mybir
from concourse.bass_utils import run_bass_kernel_spmd

F32 = mybir.dt.float32
BF16 = mybir.dt.bfloat16
AF = mybir.ActivationFunctionType
ALU = mybir.AluOpType

NCORES = 8
D = 1024
T = 512
PRE = 128
NTILE = 8
TOK = T * NTILE
HW = 32
CW = 31
NS = 5
RF = 12
RB = 8
NCM = 4
NCONV = 24
VEC_CHUNKS = ()
FUSED = True

LAYER_BLOCKS = ['u0', 'u1', 'za0', 'za1', 'v0', 'v1', 'gt0', 'ab0', 'gt1', 'ab1', 'zb0', 'zb1',
                'ga0', 'pa0', 'ga1', 'pa1', 'gb0', 'gb1', 'pb0', 'pb1', 'out0', 'out1',
                'ple', 'pg0', 'pg1']
NBL = len(LAYER_BLOCKS)
BIDX = {n: i for i, n in enumerate(LAYER_BLOCKS)}
SCR_BLOCKS = LAYER_BLOCKS + ['cm%d' % c for c in range(8)]
NSB = len(SCR_BLOCKS)
SIDX = {n: i for i, n in enumerate(SCR_BLOCKS)}
CMW = CW * 128
WIN_OFF = dict(u=0, v=1024, za=2048, ab=3072, gt=4096, zb=5120, ga=6144, gb=7168)
BCH = {k: v // 128 for k, v in WIN_OFF.items()}

C_BIN = 0
C_NG = 128
C_LNG = 144
C_LNB = 160
C_CB = 176
C_GNG = 192
C_GNB = 208
C_PLEG = 224
C_FG = 240
C_CW = 248
C_FLAG = 744
NCOL = 768


class Op:
    __slots__ = ('eng', 'fn', 'deps', 'signaled', 'count', 'sem', 'semval', 'is_dma', 'label', 'ninst')


class Prog:
    ENG = ['pe', 'act', 'dve', 'pool', 'sp']

    def __init__(self):
        self.ops = {e: [] for e in self.ENG}
        self.last_w = {}
        self.readers = {}
        self.dmacount = {}
        self.label = ''

    def add(self, eng, fn, reads=(), writes=(), sem=None, extra=()):
        op = Op()
        op.eng = eng
        op.fn = fn
        op.is_dma = sem is not None
        op.sem = sem
        op.signaled = False
        op.count = 0
        op.semval = 0
        op.label = self.label
        op.ninst = 0
        deps = set(extra)
        for r in reads:
            w = self.last_w.get(r)
            if w is not None:
                deps.add(w)
        for wk in writes:
            w = self.last_w.get(wk)
            if w is not None:
                deps.add(w)
            for rd in self.readers.get(wk, ()):
                deps.add(rd)
        op.deps = deps
        for r in reads:
            self.readers.setdefault(r, []).append(op)
        for wk in writes:
            self.last_w[wk] = op
            self.readers[wk] = []
        if op.is_dma:
            self.dmacount[sem] = self.dmacount.get(sem, 0) + 16
            op.semval = self.dmacount[sem]
        self.ops[eng].append(op)
        return op

    def finalize(self):
        for e in self.ENG:
            for op in self.ops[e]:
                for d in op.deps:
                    if d.is_dma:
                        continue
                    if d.eng == 'pe' and e == 'pe' and not op.is_dma:
                        continue
                    d.signaled = True
        for e in self.ENG:
            c = 0
            for op in self.ops[e]:
                if op.signaled and not op.is_dma:
                    c += 1
                    op.count = c

    def emit(self, e, h, engsems, dmasems):
        waited = {}
        for op in self.ops[e]:
            need = {}
            for d in op.deps:
                if d.is_dma:
                    k = ('d', d.sem)
                    v = d.semval
                else:
                    if d.eng == 'pe' and e == 'pe' and not op.is_dma:
                        continue
                    k = ('e', d.eng)
                    v = d.count
                if v > need.get(k, 0):
                    need[k] = v
            for k, v in need.items():
                if waited.get(k, 0) < v:
                    s = dmasems[k[1]] if k[0] == 'd' else engsems[k[1]]
                    h.wait_ge(s, v)
                    waited[k] = v
            if op.fn is None:
                continue
            ins = op.fn(h)
            op.ninst = getattr(h, '_n', 0)
            if op.is_dma:
                ins.then_inc(dmasems[op.sem], 16)
            elif op.signaled:
                ins.then_inc(engsems[e], 1)


def build_program(layers, final, ntile=NTILE):
    NL = len(layers)
    nc = bass.Bass("TRN2", target_bir_lowering=False)
    TOK = T * ntile
    NTOKIN = PRE + TOK

    xT_d = nc.dram_tensor("xT", [128, 8, NTOKIN], F32, kind="ExternalInput").ap()
    pT_d = nc.dram_tensor("pT", [NL, 128, 2, NTOKIN], F32, kind="ExternalInput").ap()
    wts_d = nc.dram_tensor("wts", [NL * NBL, 128, 4096], F32, kind="ExternalInput").ap()
    consts_d = nc.dram_tensor("consts", [128, NCOL], F32, kind="ExternalInput").ap()
    wsT_d = nc.dram_tensor("wsT", [128, 2 * 8 * 128], F32, kind="ExternalInput").ap()
    bsrep_d = nc.dram_tensor("bsrep", [128, 2 * 8 * 128], F32, kind="ExternalInput").ap()
    bvrow_d = nc.dram_tensor("bvrow", [1, 2 * 1024], F32, kind="ExternalInput").ap()
    cmat_d = nc.dram_tensor("cmat", [128, 128], F32, kind="ExternalInput").ap()
    wscr_d = nc.dram_tensor("wscr", [NL * NSB, 128, 4096], BF16).ap()
    out_d = nc.dram_tensor("outT", [128, 8, TOK], F32, kind="ExternalOutput").ap()

    KSTOP = os.environ.get('KSTOP', '')
    P = Prog()
    es = ExitStack()
    sb = lambda name, shape, dt: es.enter_context(nc.sbuf_tensor(name, shape, dt))
    xT = [sb("xT0", [128, 8, T], F32), sb("xT1", [128, 8, T], F32)]
    Bh = sb("Bh", [128, 8, T], BF16)
    Bu = sb("Bu", [128, 8, T], BF16)
    Bv = sb("Bv", [128, 8, T], BF16)
    By = sb("By", [128, 8, T], BF16)
    F16 = sb("F16", [128, 8, T], F32)
    wring = sb("wring", [128, NS, 4096], BF16)
    fr = sb("fr", [128, RF, T], F32)
    br = sb("br", [128, RB, T], BF16)
    hglu = sb("hglu", [128, 2, 8, HW + T], BF16)
    ubuf = ubb = None
    cmatb = sb("cmatb", [128, 128], BF16)
    pst = sb("pst", [128, NL, 2, T], F32)
    pbf = sb("pbf", [128, 2, T], BF16)
    consts = sb("consts_sb", [128, NCOL], F32)
    wsTbf = sb("wsTbf", [128, 2, 8, 128], BF16)
    Ct = sb("Ct", [128, 2, 8, 128], F32)
    bv2 = sb("bv2", [128, 2 * 1024], BF16)
    Eb = sb("Eb", [128, 128], BF16)
    cmat = sb("cmat_sb", [128, 128], F32)
    ones_bf = sb("ones_bf", [128, 128], BF16)
    j128_bf = sb("j128_bf", [128, 128], BF16)
    eps = sb("eps", [128, 2], F32)
    dummy = sb("dmy_sb", [128, 2], F32)
    cbc = sb("cbc", [128, 16], F32)
    cbh = sb("cbh", [128, 16], BF16)
    cbl = sb("cbl", [128, 16], BF16)
    cbf = sb("cbf", [128, 16], F32)
    bst = sb("bst", [128, 4, 2, 6], F32)
    bmv = sb("bmv", [128, 4, 2], F32)
    bsd = sb("bsd", [128, 4, 2], F32)
    banks = [es.enter_context(nc.psum_tensor("ps%d" % i, [128, 512], F32)) for i in range(8)]

    engsems = {e: es.enter_context(nc.semaphore("sem_" + e)) for e in ['pe', 'act', 'dve', 'pool', 'sp']}
    dmasems = {}

    def dsem(name):
        if name not in dmasems:
            dmasems[name] = es.enter_context(nc.semaphore("d_" + name))
        return name

    st = dict(bank=0, fr=0, br=0, slot=0, cm=0)

    def bank():
        i = st['bank']
        st['bank'] = (i + 1) % 8
        return banks[i], ('ps', i)

    def frs():
        i = st['fr']
        st['fr'] = (i + 1) % RF
        return fr[:, i, :], ('fr', i)

    def brs():
        i = st['br']
        st['br'] = (i + 1) % RB
        return br[:, i, :], ('br', i)

    def ccol(base, l=0, k=0):
        c = base + l * 8 + k
        return consts[:, c:c + 1]

    def bcol(l, ch):
        c = C_BIN + l * 64 + ch
        return consts[:, c:c + 1]

    ALLC = ('consts',)

    grp = []

    def gdma(out, in_, writes):
        grp.append(P.add('sp', lambda h, o=out, i=in_: h.dma_start(out=o, in_=i), writes=writes, sem=dsem('setup')))

    F16flat = F16[:].rearrange("p a t -> p (a t)")
    wst_stage = F16flat[:, 0:2048]
    bv_stage = F16flat[:, 2048:4096]
    frflat = fr[:].rearrange("p a t -> p (a t)")
    bv_hif = frflat[:, 0:2048]

    P.add('dve', lambda h: h.memset(bv_stage[0:33, :], 0.0), writes=[('F', 4), ('F', 5), ('F', 6), ('F', 7)])
    gdma(consts[:], consts_d, [ALLC])
    gdma(cmat[:], cmat_d, [('cmat',)])
    gdma(wst_stage, wsT_d, [('F', 0), ('F', 1), ('F', 2), ('F', 3)])
    gdma(Ct[:].rearrange("p a b c -> p (a b c)"), bsrep_d, [('Ct',)])
    gdma(bv_stage[0:1, :], bvrow_d, [('F', 4), ('F', 5)])
    gdma(bv_stage[32:33, :], bvrow_d, [('F', 6), ('F', 7)])
    for g_ in grp:
        g_.semval = P.dmacount['setup']

    P.add('dve', lambda h: h.tensor_copy(out=cmatb[:], in_=cmat[:]), reads=[('cmat',)], writes=[('cmatb',)])
    P.add('dve', lambda h: h.memset(ones_bf[:], 1.0), writes=[('ones',)])
    P.add('dve', lambda h: h.memset(j128_bf[:], 1.0 / 128.0), writes=[('j128',)])
    P.add('dve', lambda h: h.memset(dummy[:], 1.0), writes=[('dummy', 0), ('dummy', 1)])
    P.add('dve', lambda h: h.memset(eps[:, 0:1], 1e-6), writes=[('eps', 0)])
    P.add('dve', lambda h: h.memset(eps[:, 1:2], 1e-5), writes=[('eps', 1)])
    P.add('dve', lambda h: h.memset(hglu[:].rearrange("p a b c -> p (a b c)"), 0.0),
          writes=[('hg', l, c) for l in range(2) for c in range(8)])
    wst3 = wst_stage.rearrange("p (g i) -> p g i", i=128)
    P.add('dve', lambda h: h.memset(wst3[64:128, :, 0:64], 0.0),
          reads=[('F', 0)], writes=[('F', 0), ('F', 1), ('F', 2), ('F', 3)])
    P.add('dve', lambda h: h.tensor_copy(out=wsTbf[:].rearrange("p a b c -> p (a b c)"), in_=wst_stage),
          reads=[('F', 0), ('F', 1), ('F', 2), ('F', 3)], writes=[('wsT',)])
    for l in range(2):
        for hh in range(2):
            bk, bkk = bank()

            def f(h, bk=bk, l=l, hh=hh):
                return h.matmul(bk[:, :], lhsT=ones_bf[:],
                                rhs=wsTbf[:, l, hh * 4:(hh + 1) * 4, :].rearrange("p a b -> p (a b)"),
                                start=True, stop=True)
            P.add('pe', f, reads=[('ones',), ('wsT',)], writes=[bkk])
            for gg in range(4):
                g = hh * 4 + gg
                P.add('dve', lambda h, bk=bk, l=l, g=g, gg=gg: h.scalar_tensor_tensor(
                    out=Ct[:, l, g, :], in0=bk[:, gg * 128:(gg + 1) * 128], scalar=ccol(C_LNB, l, g),
                    in1=Ct[:, l, g, :], op0=ALU.mult, op1=ALU.add),
                    reads=[bkk, ALLC, ('Ct',)], writes=[('Ct',)])
    P.add('dve', lambda h: h.memset(Eb[:], 0.0), writes=[('Eb',)])
    P.add('dve', lambda h: h.memset(Eb[0:1, :], 1.0), reads=[('Eb',)], writes=[('Eb',)])
    P.add('dve', lambda h: h.memset(Eb[32:33, :], 1.0), reads=[('Eb',)], writes=[('Eb',)])
    P.add('dve', lambda h: h.memset(bv2[:], 0.0), writes=[('bv2',)])
    P.add('dve', lambda h: h.tensor_copy(out=bv2[0:33, :], in_=bv_stage[0:33, :]),
          reads=[('F', 4), ('F', 5), ('F', 6), ('F', 7), ('bv2',)], writes=[('bv2',)])
    P.add('dve', lambda h: h.tensor_copy(out=bv_hif[0:33, :], in_=bv2[0:33, :]),
          reads=[('bv2',)], writes=[('fr', 0), ('fr', 1), ('fr', 2), ('fr', 3)])
    P.add('dve', lambda h: h.tensor_tensor(out=bv2[32:33, :], in0=bv_stage[32:33, :], in1=bv_hif[32:33, :],
                                           op=ALU.subtract),
          reads=[('fr', 0), ('fr', 1), ('fr', 2), ('fr', 3), ('F', 6), ('F', 7)], writes=[('bv2',)])
    cbsrc = consts[:, C_CB:C_CB + 16]
    P.add('dve', lambda h: h.tensor_copy(out=cbh[:], in_=cbsrc), reads=[ALLC], writes=[('cbh',)])
    P.add('dve', lambda h: h.tensor_copy(out=cbf[:], in_=cbh[:]), reads=[('cbh',)], writes=[('cbf',)])
    P.add('dve', lambda h: h.tensor_tensor(out=cbl[:], in0=cbsrc, in1=cbf[:], op=ALU.subtract),
          reads=[('cbf',), ALLC], writes=[('cbl',)])
    bk, bkk = bank()

    def f(h, bk=bk):
        h.matmul(bk[:, 0:16], lhsT=j128_bf[:], rhs=cbh[:], start=True, stop=False)
        return h.matmul(bk[:, 0:16], lhsT=j128_bf[:], rhs=cbl[:], start=False, stop=True)
    P.add('pe', f, reads=[('j128',), ('cbh',), ('cbl',)], writes=[bkk])
    P.add('dve', lambda h, bk=bk: h.tensor_tensor(out=cbc[:], in0=cbsrc, in1=bk[:, 0:16], op=ALU.subtract),
          reads=[bkk, ALLC], writes=[('cbc',)])

    conv_ops = []
    converted = set()
    mode = {'cast': False}

    def emit_conversion(li, n):
        if (li, n) in converted or n.startswith('cm'):
            return
        converted.add((li, n))
        i = len(conv_ops)
        idx = li * NSB + SIDX[n]
        widx = li * NBL + BIDX[n]
        ncols = 2048 if n == 'ple' else 4096
        extra = [conv_ops[i - NCONV]] if i >= NCONV else []
        op = P.add('pool', lambda h: h.dma_start(out=wscr_d[idx, :, 0:ncols], in_=wts_d[widx, :, 0:ncols]),
                   writes=[('scr', idx)], sem=dsem('cv%d' % (i % NCONV)), extra=extra)
        conv_ops.append(op)

    def load_block(li, name):
        s = st['slot']
        st['slot'] = (s + 1) % NS
        idx = li * NSB + SIDX[name]
        ncols = 2048 if name == 'ple' else (CMW if name.startswith('cm') else 4096)
        if mode['cast'] and not name.startswith('cm'):
            widx = li * NBL + BIDX[name]
            P.add('pool', lambda h: h.dma_start(out=wring[:, s, 0:ncols], in_=wts_d[widx, :, 0:ncols]),
                  writes=[('w', s)], sem=dsem('wc%d' % s))
            if (li, name) not in converted:
                converted.add((li, name))
                P.add('sp', lambda h: h.dma_start(out=wscr_d[idx, :, 0:ncols], in_=wring[:, s, 0:ncols]),
                      reads=[('w', s)], writes=[('scr', idx)], sem=dsem('wb%d' % s))
            return s
        P.add('sp', lambda h: h.dma_start(out=wring[:, s, 0:ncols], in_=wscr_d[idx, :, 0:ncols]),
              reads=[('scr', idx)], writes=[('w', s)], sem=dsem('w%d' % s))
        return s

    def xkeys(xb):
        return [('x', xb, kc) for kc in range(8)]

    def norm(xb, nt, gbase, l, out, okeys, warm=0):
        P.label = 'norm%d' % gbase
        P.add('act', lambda h: h.activation(out=dummy[:, 1:2], in_=dummy[:, 0:1], func=AF.Ln),
              reads=[('dummy', 0)], writes=[('dummy', 1)])
        bk, bkk = bank()
        for kc in range(8):
            P.add('act', lambda h, kc=kc: h.activation(out=By[:, kc, :nt], in_=xT[xb][:, kc, :nt], func=AF.Square),
                  reads=[('x', xb, kc)], writes=[('By', kc)])
            P.add('pe', lambda h, kc=kc: h.matmul(bk[:, :nt], lhsT=ones_bf[:], rhs=By[:, kc, :nt],
                                                  start=(kc == 0), stop=(kc == 7)),
                  reads=[('By', kc), ('ones',)], writes=[bkk])
        if warm:
            wb, wbk = bank()

            def fw(h):
                for _ in range(warm):
                    ins = h.matmul(wb[:, :], lhsT=ones_bf[:], rhs=wsTbf[:, 0, 0:4, :].rearrange("p a b -> p (a b)"),
                                   start=True, stop=True)
                return ins
            P.add('pe', fw, reads=[('ones',), ('wsT',)], writes=[wbk])
        s, sk = frs()
        P.add('act', lambda h: h.activation(out=s[:, :nt], in_=bk[:, :nt], func=AF.Ln, bias=eps[:, 0:1],
                                            scale=1.0 / D), reads=[bkk, ('eps', 0)], writes=[sk])
        P.add('act', lambda h: h.activation(out=s[:, :nt], in_=s[:, :nt], func=AF.Exp, scale=-0.5),
              reads=[sk], writes=[sk])
        for kc in range(8):
            P.add('dve', lambda h, kc=kc: h.scalar_tensor_tensor(
                out=out[:, kc, :nt], in0=xT[xb][:, kc, :nt], scalar=ccol(gbase, l, kc), in1=s[:, :nt],
                op0=ALU.mult, op1=ALU.mult), reads=[('x', xb, kc), sk, ALLC], writes=[okeys[kc]])

    def wview(s, nk=8):
        if nk == 8:
            return wring[:, s, :].rearrange("p (k n) -> p k n", k=8)
        return wring[:, s, 0:2048].rearrange("p (k n) -> p k n", k=2)

    bgq = []

    def drain(n):
        for _ in range(n):
            if bgq:
                bgq.pop(0)()

    def proj(s, m, rhs, rkeys, nt, nk=8):
        bk, bkk = bank()
        wv = wview(s, nk)

        def f(h):
            for kc in range(nk):
                ins = h.matmul(bk[:, :nt], lhsT=wv[:, kc, m * 128:(m + 1) * 128], rhs=rhs[:, kc, :nt],
                               start=(kc == 0), stop=(kc == nk - 1))
            return ins
        P.add('pe', f, reads=[('w', s)] + rkeys, writes=[bkk])
        drain(2)
        return bk, bkk

    BhK = [('Bh', kc) for kc in range(8)]
    BuK = [('Bu', kc) for kc in range(8)]
    ByK = [('By', kc) for kc in range(8)]

    def glu_half(l, li, hh, nt):
        s = load_block(li, 'gt%d' % hh)
        sgs = []
        for m in range(4):
            c = hh * 4 + m
            bk, bkk = proj(s, m, Bh, BhK, nt)
            sg, sgk = frs()
            P.add('act', lambda h, bk=bk, sg=sg, c=c: h.activation(out=sg[:, :nt], in_=bk[:, :nt], func=AF.Sigmoid,
                                                                   bias=bcol(l, BCH['gt'] + c)),
                  reads=[bkk, ALLC], writes=[sgk])
            sgs.append((sg, sgk))
        return sgs

    def ab_half(l, li, hh, nt, sgs, hl=None):
        hl = l if hl is None else hl
        s = load_block(li, 'ab%d' % hh)
        for m in range(4):
            c = hh * 4 + m
            bk, bkk = proj(s, m, Bh, BhK, nt)
            sg, sgk = sgs[m]
            P.add('dve', lambda h, bk=bk, sg=sg, c=c: h.scalar_tensor_tensor(
                out=hglu[:, hl, c, HW:HW + nt], in0=bk[:, :nt], scalar=bcol(l, BCH['ab'] + c), in1=sg[:, :nt],
                op0=ALU.add, op1=ALU.mult), reads=[bkk, sgk, ALLC], writes=[('hg', hl, c)])

    def head_copy(l, prev_nt, use_flag):
        hk = [('hg', l, c) for c in range(8)]
        if use_flag:
            P.add('dve', lambda h: h.tensor_scalar(out=hglu[:, l, :, 0:HW], in0=hglu[:, l, :, prev_nt:prev_nt + HW],
                                                   scalar1=consts[:, C_FLAG:C_FLAG + 1], scalar2=None, op0=ALU.mult),
                  reads=hk + [ALLC], writes=hk)
        else:
            P.add('dve', lambda h: h.tensor_copy(out=hglu[:, l, :, 0:HW], in_=hglu[:, l, :, prev_nt:prev_nt + HW]),
                  reads=hk, writes=hk)

    def layer_halo(l, li, nt, xb):
        norm(xb, nt, C_NG, l, Bh, BhK)
        for hh in range(2):
            sgs = glu_half(l, li, hh, nt)
            ab_half(l, li, hh, nt, sgs)

    def conv_phase(l, li, nt, hl, jit=False):
        st1 = {}
        st2 = {}
        jit_slots = {}

        def jit_stage0(c):
            cs = st['slot']
            st['slot'] = (cs + 1) % NS
            jit_slots[c] = cs
            col0 = C_CW + (l * 8 + c) * CW
            P.add('dve', lambda h: h.tensor_tensor(
                out=wring[:, cs, 0:CMW].rearrange("p (k n) -> p k n", n=128),
                in0=cmat[:].unsqueeze(1).broadcast_to([128, CW, 128]),
                in1=consts[:, col0:col0 + CW].unsqueeze(2).broadcast_to([128, CW, 128]),
                op=ALU.mult), reads=[('cmat',), ALLC], writes=[('w', cs)])
            idx = li * NSB + SIDX['cm%d' % c]
            P.add('act', lambda h: h.dma_start(out=wscr_d[idx, :, 0:CMW], in_=wring[:, cs, 0:CMW]),
                  reads=[('w', cs)], writes=[('scr', idx)], sem=dsem('cmw%d' % cs))
        if jit and 0 not in VEC_CHUNKS:
            jit_stage0(0)
        for it in range(10):
            c = it
            if c < 8:
                bk1, bk1k = bank()
                if c in VEC_CHUNKS:
                    j = VEC_CHUNKS.index(c)
                    P.add('pe', lambda h, bk1=bk1, j=j: h.matmul(bk1[:, :nt], lhsT=cmatb[:], rhs=ubb[:, j, :nt],
                                                                 start=True, stop=True),
                          reads=[('Ub', j), ('cmatb',)], writes=[bk1k])
                elif jit:
                    if c + 1 < 8 and (c + 1) not in VEC_CHUNKS:
                        jit_stage0(c + 1)
                    cs = jit_slots[c]
                    mk = []
                else:
                    mk = []
                    cs = load_block(li, 'cm%d' % c)

                def f(h, cs=cs, bk1=bk1, c=c):
                    for k in range(CW):
                        ins = h.matmul(bk1[:, :nt], lhsT=wring[:, cs, k * 128:(k + 1) * 128],
                                       rhs=hglu[:, hl, c, 2 + k:2 + k + nt], start=(k == 0), stop=(k == CW - 1))
                    return ins
                if c not in VEC_CHUNKS:
                    P.add('pe', f, reads=[('w', cs), ('hg', hl, c)] + mk, writes=[bk1k])
                hs, hsk = brs()
                P.add('act', lambda h, bk1=bk1, hs=hs, c=c: h.activation(out=hs[:, :nt], in_=bk1[:, :nt], func=AF.Square,
                                                                         bias=cbc[:, l * 8 + c:l * 8 + c + 1]),
                      reads=[bk1k, ('cbc',)], writes=[hsk])
                st1[c] = (bk1, bk1k, hs, hsk)
            pc = it - 1
            if 0 <= pc < 8:
                pb1, pb1k, phs, phsk = st1.pop(pc)
                bk2, bk2k = bank()
                P.add('pe', lambda h, bk2=bk2, phs=phs: h.matmul(bk2[:, :nt], lhsT=j128_bf[:], rhs=phs[:, :nt],
                                                                 start=True, stop=True),
                      reads=[phsk, ('j128',)], writes=[bk2k])
                sd, sdk = frs()
                P.add('act', lambda h, bk2=bk2, sd=sd: h.activation(out=sd[:, :nt], in_=bk2[:, :nt], func=AF.Ln,
                                                                    bias=eps[:, 1:2]),
                      reads=[bk2k, ('eps', 1)], writes=[sdk])
                P.add('act', lambda h, sd=sd: h.activation(out=sd[:, :nt], in_=sd[:, :nt], func=AF.Exp, scale=-0.5),
                      reads=[sdk], writes=[sdk])
                st2[pc] = (pb1, pb1k, sd, sdk)
            qc = it - 2
            if 0 <= qc < 8:
                pb1, pb1k, sd, sdk = st2.pop(qc)
                tb, tbk = frs()
                P.add('dve', lambda h, pb1=pb1, sd=sd, tb=tb, qc=qc: h.scalar_tensor_tensor(
                    out=tb[:, :nt], in0=pb1[:, :nt], scalar=cbc[:, l * 8 + qc:l * 8 + qc + 1], in1=sd[:, :nt],
                    op0=ALU.add, op1=ALU.mult), reads=[pb1k, sdk, ('cbc',)], writes=[tbk])
                hn, hnk = brs()
                P.add('act', lambda h, tb=tb, hn=hn, qc=qc: h.activation(
                    out=hn[:, :nt], in_=tb[:, :nt], func=AF.Silu, bias=ccol(C_GNB, l, qc), scale=ccol(C_GNG, l, qc)),
                    reads=[tbk, ALLC], writes=[hnk])
                P.add('dve', lambda h, hn=hn, qc=qc: h.tensor_tensor(out=By[:, qc, :nt], in0=hn[:, :nt],
                                                                     in1=Bv[:, qc, :nt], op=ALU.mult),
                      reads=[hnk, ('Bv', qc)], writes=[('By', qc)])

    def proj4_kouter(s, rhs, rkey, nt):
        bks = [bank() for _ in range(4)]
        wv = wview(s)
        for kc in range(8):
            def f(h, kc=kc):
                for m in range(4):
                    ins = h.matmul(bks[m][0][:, :nt], lhsT=wv[:, kc, m * 128:(m + 1) * 128], rhs=rhs[:, kc, :nt],
                                   start=(kc == 0), stop=(kc == 7))
                return ins
            P.add('pe', f, reads=[('w', s), (rkey, kc)], writes=[b[1] for b in bks])
            drain(1)
        return bks

    def layer_full(l, li, nt, xb, hl=None, hooks=None, skip_norm=False, jit=False):
        hl = l if hl is None else hl
        hooks = list(hooks or [])

        def hook():
            if hooks:
                hooks.pop(0)()
        nb = nt // 128
        if not skip_norm:
            norm(xb, nt, C_NG, l, Bh, BhK, warm=(12 if nt == T else 0))
        P.label = 'u'
        for hh in range(2):
            s = load_block(li, 'u%d' % hh)
            pre4 = proj4_kouter(s, Bh, 'Bh', nt) if hh == 0 else None
            for m in range(4):
                c = hh * 4 + m
                bk, bkk = pre4[m] if pre4 else proj(s, m, Bh, BhK, nt)
                P.add('act', lambda h, bk=bk, c=c: h.activation(out=Bu[:, c, :nt], in_=bk[:, :nt],
                                                                func=AF.Gelu_apprx_tanh, bias=bcol(l, BCH['u'] + c)),
                      reads=[bkk, ALLC], writes=[('Bu', c)])
        hook()
        P.label = 'za'
        for hh in range(2):
            s = load_block(li, 'za%d' % hh)
            for m in range(4):
                c = hh * 4 + m
                bk, bkk = proj(s, m, Bh, BhK, nt)
                z, zk = brs()
                P.add('act', lambda h, bk=bk, z=z, c=c: h.activation(out=z[:, :nt], in_=bk[:, :nt], func=AF.Silu,
                                                                     bias=bcol(l, BCH['za'] + c)),
                      reads=[bkk, ALLC], writes=[zk])
                P.add('dve', lambda h, z=z, c=c: h.tensor_tensor(out=Bu[:, c, :nt], in0=Bu[:, c, :nt], in1=z[:, :nt],
                                                                 op=ALU.mult),
                      reads=[zk, ('Bu', c)], writes=[('Bu', c)])
        hook()
        P.label = 'v'
        gv = F16[:].rearrange("p (b h) t -> p b (h t)", h=2)
        vhat = Bv[:].rearrange("p (b h) t -> p b (h t)", h=2)
        for hh in range(2):
            s = load_block(li, 'v%d' % hh)
            wv = wview(s)
            for b in range(nb):
                bk, bkk = bank()

                def f(h, bk=bk, wv=wv, b=b, hh=hh):
                    for kc in range(8):
                        h.matmul(bk[:, :], lhsT=Bh[:, kc, b * 128:(b + 1) * 128], rhs=wv[:, kc, :],
                                 start=(kc == 0), stop=False)
                    o = l * 1024 + hh * 512
                    return h.matmul(bk[:, :], lhsT=Eb[:], rhs=bv2[:, o:o + 512], start=False, stop=True)
                P.add('pe', f, reads=[('w', s), ('Eb',), ('bv2',)] + BhK, writes=[bkk])
                P.add('act', lambda h, bk=bk, b=b, hh=hh: h.activation(out=gv[:, b, hh * 512:(hh + 1) * 512], in_=bk[:, :],
                                                                       func=AF.Gelu_apprx_tanh),
                      reads=[bkk], writes=[('F', 2 * b + hh)])
        for b in range(nb):
            fk = [('F', 2 * b), ('F', 2 * b + 1)]
            for a in range(2):
                P.add('dve', lambda h, b=b, a=a: h.bn_stats(out=bst[:, b, a, :], in_=gv[:, b, a * 512:(a + 1) * 512]),
                      reads=[fk[a]], writes=[('bst', b, a)])
            P.add('dve', lambda h, b=b: h.bn_aggr(out=bmv[:, b, :], in_=bst[:, b, :, :].rearrange("p a n -> p (a n)")),
                  reads=[('bst', b, 0), ('bst', b, 1)], writes=[('bmv', b)])
            P.add('act', lambda h, b=b: h.activation(out=bsd[:, b, 0:1], in_=bmv[:, b, 1:2], func=AF.Sqrt,
                                                     bias=eps[:, 1:2]),
                  reads=[('bmv', b), ('eps', 1)], writes=[('bsd', b)])
            P.add('dve', lambda h, b=b: h.reciprocal(out=bsd[:, b, 0:1], in_=bsd[:, b, 0:1]),
                  reads=[('bsd', b)], writes=[('bsd', b)])
            P.add('dve', lambda h, b=b: h.scalar_tensor_tensor(out=bsd[:, b, 1:2], in0=bmv[:, b, 0:1], scalar=-1.0,
                                                               in1=bsd[:, b, 0:1], op0=ALU.mult, op1=ALU.mult),
                  reads=[('bsd', b), ('bmv', b)], writes=[('bsd2', b)])
            P.add('act', lambda h, b=b: h.activation(out=vhat[:, b, :], in_=gv[:, b, :], func=AF.Identity,
                                                     bias=bsd[:, b, 1:2], scale=bsd[:, b, 0:1]),
                  reads=fk + [('bsd', b), ('bsd2', b)], writes=[('Bv', 2 * b), ('Bv', 2 * b + 1)])
        hook()
        P.label = 'gt0'
        sgs0 = glu_half(l, li, 0, nt)
        P.label = 'glu'
        ab_half(l, li, 0, nt, sgs0, hl)
        for j, c in enumerate(VEC_CHUNKS):
            for k in range(CW):
                col = C_CW + (l * 8 + c) * CW + k
                if k == 0:
                    bgq.append(lambda j=j, c=c, col=col: P.add('dve', lambda h: h.tensor_scalar(
                        out=ubuf[:, j, :nt], in0=hglu[:, hl, c, 2:2 + nt], scalar1=consts[:, col:col + 1], scalar2=None,
                        op0=ALU.mult), reads=[('hg', hl, c), ALLC], writes=[('U', j)]))
                else:
                    bgq.append(lambda j=j, c=c, col=col, k=k: P.add('dve', lambda h: h.scalar_tensor_tensor(
                        out=ubuf[:, j, :nt], in0=hglu[:, hl, c, 2 + k:2 + k + nt], scalar=consts[:, col:col + 1],
                        in1=ubuf[:, j, :nt], op0=ALU.mult, op1=ALU.add), reads=[('hg', hl, c), ALLC, ('U', j)],
                        writes=[('U', j)]))
            bgq.append(lambda j=j: P.add('act', lambda h: h.activation(out=ubb[:, j, :nt], in_=ubuf[:, j, :nt],
                                                                       func=AF.Copy),
                                         reads=[('U', j)], writes=[('Ub', j)]))
        hook()
        P.label = 'mix'
        for g in range(8):
            bk, bkk = bank()

            def f(h, bk=bk, g=g):
                for b in range(nb):
                    ins = h.matmul(bk[:, b * 128:(b + 1) * 128], lhsT=vhat[:, b, g * 128:(g + 1) * 128],
                                   rhs=wsTbf[:, l, g, :], start=True, stop=True)
                return ins
            P.add('pe', f, reads=[('Bv', i) for i in range(2 * nb)] + [('wsT',)], writes=[bkk])
            ta, tak = frs()
            P.add('dve', lambda h, bk=bk, ta=ta, g=g: h.scalar_tensor_tensor(
                out=ta[:, :nt].rearrange("p (b i) -> p b i", i=128),
                in0=bk[:, :nt].rearrange("p (b i) -> p b i", i=128),
                scalar=ccol(C_LNG, l, g),
                in1=Ct[:, l, g, :].unsqueeze(1).broadcast_to([128, nb, 128]),
                op0=ALU.mult, op1=ALU.add), reads=[bkk, ('Ct',), ALLC], writes=[tak])
            P.add('dve', lambda h, ta=ta, g=g: h.tensor_tensor(out=Bu[:, g, :nt], in0=ta[:, :nt], in1=Bu[:, g, :nt],
                                                               op=ALU.mult),
                  reads=[tak, ('Bu', g)], writes=[('Bu', g)])
        P.label = 'glu'
        sgs1 = glu_half(l, li, 1, nt)
        ab_half(l, li, 1, nt, sgs1, hl)
        P.label = 'zb'
        for hh in range(2):
            s = load_block(li, 'zb%d' % hh)
            for m in range(4):
                c = hh * 4 + m
                bk, bkk = proj(s, m, Bh, BhK, nt)
                P.add('act', lambda h, bk=bk, c=c: h.activation(out=Bv[:, c, :nt], in_=bk[:, :nt], func=AF.Silu,
                                                                bias=bcol(l, BCH['zb'] + c)),
                      reads=[bkk, ALLC], writes=[('Bv', c)])
        hook()
        P.label = 'gapa'
        for hh in range(2):
            s = load_block(li, 'ga%d' % hh)
            sig = []
            for m in range(4):
                c = hh * 4 + m
                bk, bkk = proj(s, m, Bh, BhK, nt)
                sa, sak = frs()
                P.add('act', lambda h, bk=bk, sa=sa, c=c: h.activation(out=sa[:, :nt], in_=bk[:, :nt], func=AF.Sigmoid,
                                                                       bias=bcol(l, BCH['ga'] + c)),
                      reads=[bkk, ALLC], writes=[sak])
                sig.append((sa, sak))
            s2 = load_block(li, 'pa%d' % hh)
            pre4 = proj4_kouter(s2, Bu, 'Bu', nt) if hh == 0 else None
            for m in range(4):
                c = hh * 4 + m
                bk, bkk = pre4[m] if pre4 else proj(s2, m, Bu, BuK, nt)
                sa, sak = sig[m]
                P.add('dve', lambda h, bk=bk, sa=sa, c=c: h.tensor_tensor(out=F16[:, c, :nt], in0=bk[:, :nt],
                                                                          in1=sa[:, :nt], op=ALU.mult),
                      reads=[bkk, sak], writes=[('F', c)])
        hook()
        P.label = 'conv'
        drain(10 ** 6)
        conv_phase(l, li, nt, hl, jit)
        hook()
        P.label = 'gbpb'
        sig = []
        for hh in range(2):
            s = load_block(li, 'gb%d' % hh)
            for m in range(4):
                c = hh * 4 + m
                bk, bkk = proj(s, m, Bh, BhK, nt)
                sa, sak = frs()
                P.add('act', lambda h, bk=bk, sa=sa, c=c: h.activation(out=sa[:, :nt], in_=bk[:, :nt], func=AF.Sigmoid,
                                                                       bias=bcol(l, BCH['gb'] + c)),
                      reads=[bkk, ALLC], writes=[sak])
                sig.append((sa, sak))
        for hh in range(2):
            s2 = load_block(li, 'pb%d' % hh)
            pre4 = proj4_kouter(s2, By, 'By', nt) if hh == 0 else None
            for m in range(4):
                c = hh * 4 + m
                bk, bkk = pre4[m] if pre4 else proj(s2, m, By, ByK, nt)
                sa, sak = sig[c]
                mb, mbk = frs()
                P.add('dve', lambda h, bk=bk, sa=sa, mb=mb: h.tensor_tensor(out=mb[:, :nt], in0=bk[:, :nt],
                                                                            in1=sa[:, :nt], op=ALU.mult),
                      reads=[bkk, sak], writes=[mbk])
                P.add('dve', lambda h, mb=mb, c=c: h.tensor_tensor(out=Bu[:, c, :nt], in0=F16[:, c, :nt],
                                                                   in1=mb[:, :nt], op=ALU.add),
                      reads=[mbk, ('F', c)], writes=[('Bu', c)])
        hook()
        P.label = 'out'
        for hh in range(2):
            s = load_block(li, 'out%d' % hh)
            pre4 = proj4_kouter(s, Bu, 'Bu', nt) if hh == 0 else None
            for m in range(4):
                c = hh * 4 + m
                bk, bkk = pre4[m] if pre4 else proj(s, m, Bu, BuK, nt)
                P.add('dve', lambda h, bk=bk, c=c: h.tensor_tensor(out=xT[xb][:, c, :nt], in0=xT[xb][:, c, :nt],
                                                                   in1=bk[:, :nt], op=ALU.add),
                      reads=[bkk, ('x', xb, c)], writes=[('x', xb, c)])
        P.label = 'ple0'
        P.label = 'ple'
        P.add('act', lambda h: h.activation(out=pbf[:, :, :nt], in_=pst[:, li, :, :nt], func=AF.Copy),
              reads=[('pst', li)], writes=[('pbf',)])
        sp_ = load_block(li, 'ple')
        for c in range(8):
            bk2, bk2k = proj(sp_, c, pbf, [('pbf',)], nt, nk=2)
            if c % 2 == 0:
                P.add('act', lambda h, bk2=bk2, c=c: h.activation(out=F16[:, c, :nt], in_=bk2[:, :nt], func=AF.Copy),
                      reads=[bk2k], writes=[('F', c)])
            else:
                P.add('dve', lambda h, bk2=bk2, c=c: h.tensor_copy(out=F16[:, c, :nt], in_=bk2[:, :nt]),
                      reads=[bk2k], writes=[('F', c)])
        norm(xb, nt, C_PLEG, l, Bh, BhK, warm=(12 if nt == T else 0))
        P.label = 'ple'
        for hh in range(2):
            s = load_block(li, 'pg%d' % hh)
            pre4 = proj4_kouter(s, Bh, 'Bh', nt) if hh == 0 else None
            for m in range(4):
                c = hh * 4 + m
                bk, bkk = pre4[m] if pre4 else proj(s, m, Bh, BhK, nt)
                pg, pgk = frs()
                P.add('act', lambda h, bk=bk, pg=pg: h.activation(out=pg[:, :nt], in_=bk[:, :nt], func=AF.Sigmoid),
                      reads=[bkk], writes=[pgk])
                P.add('dve', lambda h, pg=pg, c=c: h.tensor_tensor(out=pg[:, :nt], in0=F16[:, c, :nt],
                                                                   in1=pg[:, :nt], op=ALU.mult),
                      reads=[pgk, ('F', c)], writes=[pgk])
                P.add('dve', lambda h, pg=pg, c=c: h.tensor_tensor(out=xT[xb][:, c, :nt], in0=xT[xb][:, c, :nt],
                                                                   in1=pg[:, :nt], op=ALU.add),
                      reads=[pgk, ('x', xb, c)], writes=[('x', xb, c)])

    tiles = [('pre', PRE, 0, 0)] + [(i, T, PRE + i * T, (i + 1) % 2) for i in range(ntile)]
    FK = [('F', i) for i in range(8)]
    prev_nt = None
    last_store = None

    def xload(nt, col0, xb):
        P.add('sp', lambda h: h.dma_start(out=xT[xb][:, :, :nt], in_=xT_d[:, :, col0:col0 + nt]),
              writes=xkeys(xb), sem=dsem('x%d' % xb))

    def pload(nt, col0, lis):
        for li_ in lis:
            P.add('sp', lambda h, li_=li_: h.dma_start(out=pst[:, li_, :, :nt], in_=pT_d[li_, :, :, col0:col0 + nt]),
                  writes=[('pst', li_)], sem=dsem('p%d' % li_))

    def finish(tid, xb, nt):
        t0 = tid * T
        if final:
            norm(xb, nt, C_FG, 0, F16, FK)
            return P.add('pool', lambda h: h.dma_start(out=out_d[:, :, t0:t0 + T], in_=F16[:]), reads=FK, sem=dsem('st'))
        return P.add('pool', lambda h: h.dma_start(out=out_d[:, :, t0:t0 + T], in_=xT[xb][:]), reads=xkeys(xb),
                     sem=dsem('st'))

    if NL == 2 and tiles and len(tiles) > 1:
        l0, l1 = layers
        xload(PRE, 0, 0)
        xload(T, PRE, 1)
        pload(T, PRE, [0, 1])
        mode['cast'] = True
        layer_halo(l0, 0, PRE, 0)
        head_copy(l0, PRE, True)
        layer_full(l0, 0, T, 1, jit=True)
        mode['cast'] = False
        pload(PRE, 0, [0])
        layer_full(l0, 0, PRE, 0, hl=l1)
        mode['cast'] = True
        layer_halo(l1, 1, PRE, 0)
        head_copy(l1, PRE, True)
        layer_full(l1, 1, T, 1, jit=True)
        mode['cast'] = False
        for n_ in LAYER_BLOCKS:
            emit_conversion(0, n_)
            emit_conversion(1, n_)
        tiles = tiles[2:]

        def start_next(k):
            if k < len(tiles):
                _tid, _nt, _col0, _xb = tiles[k]
                xload(_nt, _col0, _xb)
                pload(_nt, _col0, range(NL))
                norm(_xb, _nt, C_NG, l0, Bh, BhK)
        start_next(0)
        last_store = finish(0, 1, T)
        for ti, (tid, nt, col0, xb) in enumerate(tiles):
            head_copy(l0, T, False)
            layer_full(l0, 0, nt, xb, skip_norm=True)
            head_copy(l1, T, False)
            layer_full(l1, 1, nt, xb)
            start_next(ti + 1)
            last_store = finish(tid, xb, nt)
        tiles = []
    first_flag = True
    for ti, (tid, nt, col0, xb) in enumerate(tiles):
        xload(nt, col0, xb)
        pload(nt, col0, range(NL))
        for li, l in enumerate(layers):
            if tid != 'pre':
                head_copy(l, prev_nt, first_flag and ti == 1)
            if tid == 'pre' and li == NL - 1:
                layer_halo(l, li, nt, xb)
            else:
                layer_full(l, li, nt, xb)
        if tid != 'pre':
            last_store = finish(tid, xb, nt)
        prev_nt = nt
    if last_store is None:
        last_store = P.add('pool', lambda h: h.dma_start(out=out_d[:, :, 0:T], in_=F16[:]), reads=FK, sem=dsem('st'))
    P.add('pool', None, extra=[last_store] + conv_ops[-NCONV:])

    P.finalize()
    with nc.Block() as block:
        @block.tensor
        def _(h):
            P.emit('pe', h, engsems, dmasems)

        @block.scalar
        def _(h):
            P.emit('act', h, engsems, dmasems)

        @block.vector
        def _(h):
            P.emit('dve', h, engsems, dmasems)

        @block.gpsimd
        def _(h):
            P.emit('pool', h, engsems, dmasems)

        @block.sync
        def _(h):
            P.emit('sp', h, engsems, dmasems)
    es.close()
    nc._dbg_prog = P
    return nc


def _wblock(w, c0):
    return np.ascontiguousarray(w[:, c0:c0 + 512].reshape(8, 128, 512).transpose(1, 0, 2)).reshape(128, 4096)


def _layer_blocks(inp, l):
    out = np.zeros((NBL, 128, 4096), np.float32)
    for i, n in enumerate(LAYER_BLOCKS):
        if n == 'ple':
            w = inp['w_ple'][l]
            out[i, :, 0:2048] = np.ascontiguousarray(w.reshape(2, 128, 1024).transpose(1, 0, 2)).reshape(128, 2048)
            continue
        base, hh = n[:-1], int(n[-1])
        if base in WIN_OFF:
            out[i] = _wblock(inp['w_in'][l], WIN_OFF[base] + hh * 512)
        else:
            w = {'pa': inp['w_pa'], 'pb': inp['w_pb'], 'out': inp['w_out'], 'pg': inp['w_ple_gate']}[base][l]
            out[i] = _wblock(w, hh * 512)
    return out


def _colT(v):
    return np.ascontiguousarray(np.asarray(v, np.float32).reshape(-1, 128).T)


def _consts(inp, flag):
    c = np.zeros((128, NCOL), np.float32)
    for l in range(2):
        c[:, C_BIN + l * 64:C_BIN + (l + 1) * 64] = _colT(inp['b_in'][l])
        for base, key in ((C_NG, 'norm_g'), (C_LNG, 'a_ln_g'), (C_LNB, 'a_ln_b'), (C_CB, 'b_conv_b'),
                          (C_GNG, 'b_gn_g'), (C_GNB, 'b_gn_b'), (C_PLEG, 'ple_norm_g')):
            c[:, base + l * 8:base + (l + 1) * 8] = _colT(inp[key][l])
        cw = np.asarray(inp['b_conv_w'][l], np.float32)
        c[:, C_CW + l * 8 * CW:C_CW + (l + 1) * 8 * CW] = cw.reshape(CW, 8, 128).transpose(2, 1, 0).reshape(128, 8 * CW)
    c[:, C_FG:C_FG + 8] = _colT(inp['final_g'])
    c[:, C_FLAG] = flag
    return c


def _feature_major(a):
    nt = a.shape[0]
    return np.ascontiguousarray(a.reshape(nt, -1, 128).transpose(2, 1, 0))


_PROG_CACHE = {}


def _get_prog(layers, final, ntile):
    key = (tuple(layers), final, ntile)
    if key not in _PROG_CACHE:
        _PROG_CACHE[key] = build_program(list(layers), final, ntile)
    return _PROG_CACHE[key]


def _make_in_maps(layers, xs, inp):
    B, S, _ = xs.shape
    half = S // 2
    wts = np.concatenate([_layer_blocks(inp, l) for l in layers], axis=0)
    a_ws = np.asarray(inp['a_ws'], np.float32)
    wsT = np.ascontiguousarray(a_ws.transpose(3, 0, 1, 2)).reshape(128, 2 * 8 * 128)
    bsrep = np.ascontiguousarray(np.broadcast_to(np.asarray(inp['a_bs'], np.float32).reshape(1, -1), (128, 2 * 8 * 128)))
    bvrow = np.ascontiguousarray(np.asarray(inp['b_in'], np.float32)[:, 1024:2048].reshape(1, 2048))
    cmat = np.eye(128, dtype=np.float32) - np.float32(1.0 / 128.0)
    p = np.asarray(inp['p'], np.float32)
    in_maps = []
    for core in range(2 * B):
        b, hf = core // 2, core % 2
        s0 = hf * half
        xc = np.zeros((PRE + half, D), np.float32)
        xc[PRE:] = xs[b, s0:s0 + half]
        pc = np.zeros((len(layers), PRE + half, 256), np.float32)
        for li, l in enumerate(layers):
            pc[li, PRE:] = p[l, b, s0:s0 + half]
        if hf == 1:
            xc[:PRE] = xs[b, s0 - PRE:s0]
            for li, l in enumerate(layers):
                pc[li, :PRE] = p[l, b, s0 - PRE:s0]
        pT = np.stack([_feature_major(pc[li]) for li in range(len(layers))], axis=0)
        in_maps.append(dict(xT=_feature_major(xc), pT=pT, wts=wts, consts=_consts(inp, float(hf)),
                            wsT=wsT, bsrep=bsrep, bvrow=bvrow, cmat=cmat))
    return in_maps


def _gather(results, B, S):
    half = S // 2
    out = np.empty((B, S, D), np.float32)
    for core in range(2 * B):
        b, hf = core // 2, core % 2
        o = np.asarray(results[core]["outT"])
        out[b, hf * half:(hf + 1) * half] = o.transpose(2, 1, 0).reshape(half, D)
    return out


def _run(layers, final, xs, inp):
    B, S, _ = xs.shape
    nc = _get_prog(layers, final, S // 2 // T)
    in_maps = _make_in_maps(layers, xs, inp)
    res = run_bass_kernel_spmd(nc, in_maps, core_ids=list(range(2 * B)))
    return _gather(res.results, B, S)


def kernel(**inputs):
    inp = {k: np.asarray(v) for k, v in inputs.items()}
    x = np.asarray(inp['x'], np.float32)
    if FUSED:
        return _run([0, 1], True, x, inp)
    x1 = _run([0], False, x, inp)
    return _run([1], True, x1, inp)
```

```python
import os
import numpy as np
from contextlib import ExitStack
import concourse.bass as bass
import concourse.mybir as mybir
from concourse.bass_utils import run_bass_kernel_spmd

F32 = mybir.dt.float32
BF16 = mybir.dt.bfloat16
AF = mybir.ActivationFunctionType
ALU = mybir.AluOpType

NCORES = 8
D = 1024
T = 512
PRE = 128
NTILE = 8
TOK = T * NTILE
HW = 32
CW = 31
NS = 5
RF = 12
RB = 8
NCM = 4
NCONV = 24
VEC_CHUNKS = ()
FUSED = True

LAYER_BLOCKS = ['u0', 'u1', 'za0', 'za1', 'v0', 'v1', 'gt0', 'ab0', 'gt1', 'ab1', 'zb0', 'zb1',
                'ga0', 'pa0', 'ga1', 'pa1', 'gb0', 'gb1', 'pb0', 'pb1', 'out0', 'out1',
                'ple', 'pg0', 'pg1']
NBL = len(LAYER_BLOCKS)
BIDX = {n: i for i, n in enumerate(LAYER_BLOCKS)}
SCR_BLOCKS = LAYER_BLOCKS + ['cm%d' % c for c in range(8)]
NSB = len(SCR_BLOCKS)
SIDX = {n: i for i, n in enumerate(SCR_BLOCKS)}
CMW = CW * 128
WIN_OFF = dict(u=0, v=1024, za=2048, ab=3072, gt=4096, zb=5120, ga=6144, gb=7168)
BCH = {k: v // 128 for k, v in WIN_OFF.items()}

C_BIN = 0
C_NG = 128
C_LNG = 144
C_LNB = 160
C_CB = 176
C_GNG = 192
C_GNB = 208
C_PLEG = 224
C_FG = 240
C_CW = 248
C_FLAG = 744
NCOL = 768


class Op:
    __slots__ = ('eng', 'fn', 'deps', 'signaled', 'count', 'sem', 'semval', 'is_dma', 'label', 'ninst')


class Prog:
    ENG = ['pe', 'act', 'dve', 'pool', 'sp']

    def __init__(self):
        self.ops = {e: [] for e in self.ENG}
        self.last_w = {}
        self.readers = {}
        self.dmacount = {}
        self.label = ''

    def add(self, eng, fn, reads=(), writes=(), sem=None, extra=()):
        op = Op()
        op.eng = eng
        op.fn = fn
        op.is_dma = sem is not None
        op.sem = sem
        op.signaled = False
        op.count = 0
        op.semval = 0
        op.label = self.label
        op.ninst = 0
        deps = set(extra)
        for r in reads:
            w = self.last_w.get(r)
            if w is not None:
                deps.add(w)
        for wk in writes:
            w = self.last_w.get(wk)
            if w is not None:
                deps.add(w)
            for rd in self.readers.get(wk, ()):
                deps.add(rd)
        op.deps = deps
        for r in reads:
            self.readers.setdefault(r, []).append(op)
        for wk in writes:
            self.last_w[wk] = op
            self.readers[wk] = []
        if op.is_dma:
            self.dmacount[sem] = self.dmacount.get(sem, 0) + 16
            op.semval = self.dmacount[sem]
        self.ops[eng].append(op)
        return op

    def finalize(self):
        for e in self.ENG:
            for op in self.ops[e]:
                for d in op.deps:
                    if d.is_dma:
                        continue
                    if d.eng == 'pe' and e == 'pe' and not op.is_dma:
                        continue
                    d.signaled = True
        for e in self.ENG:
            c = 0
            for op in self.ops[e]:
                if op.signaled and not op.is_dma:
                    c += 1
                    op.count = c

    def emit(self, e, h, engsems, dmasems):
        waited = {}
        for op in self.ops[e]:
            need = {}
            for d in op.deps:
                if d.is_dma:
                    k = ('d', d.sem)
                    v = d.semval
                else:
                    if d.eng == 'pe' and e == 'pe' and not op.is_dma:
                        continue
                    k = ('e', d.eng)
                    v = d.count
                if v > need.get(k, 0):
                    need[k] = v
            for k, v in need.items():
                if waited.get(k, 0) < v:
                    s = dmasems[k[1]] if k[0] == 'd' else engsems[k[1]]
                    h.wait_ge(s, v)
                    waited[k] = v
            if op.fn is None:
                continue
            ins = op.fn(h)
            op.ninst = getattr(h, '_n', 0)
            if op.is_dma:
                ins.then_inc(dmasems[op.sem], 16)
            elif op.signaled:
                ins.then_inc(engsems[e], 1)


def build_program(layers, final, ntile=NTILE):
    NL = len(layers)
    nc = bass.Bass("TRN2", target_bir_lowering=False)
    TOK = T * ntile
    NTOKIN = PRE + TOK

    xT_d = nc.dram_tensor("xT", [128, 8, NTOKIN], F32, kind="ExternalInput").ap()
    pT_d = nc.dram_tensor("pT", [NL, 128, 2, NTOKIN], F32, kind="ExternalInput").ap()
    wts_d = nc.dram_tensor("wts", [NL * NBL, 128, 4096], F32, kind="ExternalInput").ap()
    consts_d = nc.dram_tensor("consts", [128, NCOL], F32, kind="ExternalInput").ap()
    wsT_d = nc.dram_tensor("wsT", [128, 2 * 8 * 128], F32, kind="ExternalInput").ap()
    bsrep_d = nc.dram_tensor("bsrep", [128, 2 * 8 * 128], F32, kind="ExternalInput").ap()
    bvrow_d = nc.dram_tensor("bvrow", [1, 2 * 1024], F32, kind="ExternalInput").ap()
    cmat_d = nc.dram_tensor("cmat", [128, 128], F32, kind="ExternalInput").ap()
    wscr_d = nc.dram_tensor("wscr", [NL * NSB, 128, 4096], BF16).ap()
    out_d = nc.dram_tensor("outT", [128, 8, TOK], F32, kind="ExternalOutput").ap()

    KSTOP = os.environ.get('KSTOP', '')
    P = Prog()
    es = ExitStack()
    sb = lambda name, shape, dt: es.enter_context(nc.sbuf_tensor(name, shape, dt))
    xT = [sb("xT0", [128, 8, T], F32), sb("xT1", [128, 8, T], F32)]
    Bh = sb("Bh", [128, 8, T], BF16)
    Bu = sb("Bu", [128, 8, T], BF16)
    Bv = sb("Bv", [128, 8, T], BF16)
    By = sb("By", [128, 8, T], BF16)
    F16 = sb("F16", [128, 8, T], F32)
    wring = sb("wring", [128, NS, 4096], BF16)
    fr = sb("fr", [128, RF, T], F32)
    br = sb("br", [128, RB, T], BF16)
    hglu = sb("hglu", [128, 2, 8, HW + T], BF16)
    ubuf = ubb = None
    cmatb = sb("cmatb", [128, 128], BF16)
    pst = sb("pst", [128, NL, 2, T], F32)
    pbf = sb("pbf", [128, 2, T], BF16)
    consts = sb("consts_sb", [128, NCOL], F32)
    wsTbf = sb("wsTbf", [128, 2, 8, 128], BF16)
    Ct = sb("Ct", [128, 2, 8, 128], F32)
    bv2 = sb("bv2", [128, 2 * 1024], BF16)
    Eb = sb("Eb", [128, 128], BF16)
    cmat = sb("cmat_sb", [128, 128], F32)
    ones_bf = sb("ones_bf", [128, 128], BF16)
    j128_bf = sb("j128_bf", [128, 128], BF16)
    eps = sb("eps", [128, 2], F32)
    dummy = sb("dmy_sb", [128, 2], F32)
    cbc = sb("cbc", [128, 16], F32)
    cbh = sb("cbh", [128, 16], BF16)
    cbl = sb("cbl", [128, 16], BF16)
    cbf = sb("cbf", [128, 16], F32)
    bst = sb("bst", [128, 4, 2, 6], F32)
    bmv = sb("bmv", [128, 4, 2], F32)
    bsd = sb("bsd", [128, 4, 2], F32)
    banks = [es.enter_context(nc.psum_tensor("ps%d" % i, [128, 512], F32)) for i in range(8)]

    engsems = {e: es.enter_context(nc.semaphore("sem_" + e)) for e in ['pe', 'act', 'dve', 'pool', 'sp']}
    dmasems = {}

    def dsem(name):
        if name not in dmasems:
            dmasems[name] = es.enter_context(nc.semaphore("d_" + name))
        return name

    st = dict(bank=0, fr=0, br=0, slot=0, cm=0)

    def bank():
        i = st['bank']
        st['bank'] = (i + 1) % 8
        return banks[i], ('ps', i)

    def frs():
        i = st['fr']
        st['fr'] = (i + 1) % RF
        return fr[:, i, :], ('fr', i)

    def brs():
        i = st['br']
        st['br'] = (i + 1) % RB
        return br[:, i, :], ('br', i)

    def ccol(base, l=0, k=0):
        c = base + l * 8 + k
        return consts[:, c:c + 1]

    def bcol(l, ch):
        c = C_BIN + l * 64 + ch
        return consts[:, c:c + 1]

    ALLC = ('consts',)

    grp = []

    def gdma(out, in_, writes):
        grp.append(P.add('sp', lambda h, o=out, i=in_: h.dma_start(out=o, in_=i), writes=writes, sem=dsem('setup')))

    F16flat = F16[:].rearrange("p a t -> p (a t)")
    wst_stage = F16flat[:, 0:2048]
    bv_stage = F16flat[:, 2048:4096]
    frflat = fr[:].rearrange("p a t -> p (a t)")
    bv_hif = frflat[:, 0:2048]

    P.add('dve', lambda h: h.memset(bv_stage[0:33, :], 0.0), writes=[('F', 4), ('F', 5), ('F', 6), ('F', 7)])
    gdma(consts[:], consts_d, [ALLC])
    gdma(cmat[:], cmat_d, [('cmat',)])
    gdma(wst_stage, wsT_d, [('F', 0), ('F', 1), ('F', 2), ('F', 3)])
    gdma(Ct[:].rearrange("p a b c -> p (a b c)"), bsrep_d, [('Ct',)])
    gdma(bv_stage[0:1, :], bvrow_d, [('F', 4), ('F', 5)])
    gdma(bv_stage[32:33, :], bvrow_d, [('F', 6), ('F', 7)])
    for g_ in grp:
        g_.semval = P.dmacount['setup']

    P.add('dve', lambda h: h.tensor_copy(out=cmatb[:], in_=cmat[:]), reads=[('cmat',)], writes=[('cmatb',)])
    P.add('dve', lambda h: h.memset(ones_bf[:], 1.0), writes=[('ones',)])
    P.add('dve', lambda h: h.memset(j128_bf[:], 1.0 / 128.0), writes=[('j128',)])
    P.add('dve', lambda h: h.memset(dummy[:], 1.0), writes=[('dummy', 0), ('dummy', 1)])
    P.add('dve', lambda h: h.memset(eps[:, 0:1], 1e-6), writes=[('eps', 0)])
    P.add('dve', lambda h: h.memset(eps[:, 1:2], 1e-5), writes=[('eps', 1)])
    P.add('dve', lambda h: h.memset(hglu[:].rearrange("p a b c -> p (a b c)"), 0.0),
          writes=[('hg', l, c) for l in range(2) for c in range(8)])
    wst3 = wst_stage.rearrange("p (g i) -> p g i", i=128)
    P.add('dve', lambda h: h.memset(wst3[64:128, :, 0:64], 0.0),
          reads=[('F', 0)], writes=[('F', 0), ('F', 1), ('F', 2), ('F', 3)])
    P.add('dve', lambda h: h.tensor_copy(out=wsTbf[:].rearrange("p a b c -> p (a b c)"), in_=wst_stage),
          reads=[('F', 0), ('F', 1), ('F', 2), ('F', 3)], writes=[('wsT',)])
    for l in range(2):
        for hh in range(2):
            bk, bkk = bank()

            def f(h, bk=bk, l=l, hh=hh):
                return h.matmul(bk[:, :], lhsT=ones_bf[:],
                                rhs=wsTbf[:, l, hh * 4:(hh + 1) * 4, :].rearrange("p a b -> p (a b)"),
                                start=True, stop=True)
            P.add('pe', f, reads=[('ones',), ('wsT',)], writes=[bkk])
            for gg in range(4):
                g = hh * 4 + gg
                P.add('dve', lambda h, bk=bk, l=l, g=g, gg=gg: h.scalar_tensor_tensor(
                    out=Ct[:, l, g, :], in0=bk[:, gg * 128:(gg + 1) * 128], scalar=ccol(C_LNB, l, g),
                    in1=Ct[:, l, g, :], op0=ALU.mult, op1=ALU.add),
                    reads=[bkk, ALLC, ('Ct',)], writes=[('Ct',)])
    P.add('dve', lambda h: h.memset(Eb[:], 0.0), writes=[('Eb',)])
    P.add('dve', lambda h: h.memset(Eb[0:1, :], 1.0), reads=[('Eb',)], writes=[('Eb',)])
    P.add('dve', lambda h: h.memset(Eb[32:33, :], 1.0), reads=[('Eb',)], writes=[('Eb',)])
    P.add('dve', lambda h: h.memset(bv2[:], 0.0), writes=[('bv2',)])
    P.add('dve', lambda h: h.tensor_copy(out=bv2[0:33, :], in_=bv_stage[0:33, :]),
          reads=[('F', 4), ('F', 5), ('F', 6), ('F', 7), ('bv2',)], writes=[('bv2',)])
    P.add('dve', lambda h: h.tensor_copy(out=bv_hif[0:33, :], in_=bv2[0:33, :]),
          reads=[('bv2',)], writes=[('fr', 0), ('fr', 1), ('fr', 2), ('fr', 3)])
    P.add('dve', lambda h: h.tensor_tensor(out=bv2[32:33, :], in0=bv_stage[32:33, :], in1=bv_hif[32:33, :],
                                           op=ALU.subtract),
          reads=[('fr', 0), ('fr', 1), ('fr', 2), ('fr', 3), ('F', 6), ('F', 7)], writes=[('bv2',)])
    cbsrc = consts[:, C_CB:C_CB + 16]
    P.add('dve', lambda h: h.tensor_copy(out=cbh[:], in_=cbsrc), reads=[ALLC], writes=[('cbh',)])
    P.add('dve', lambda h: h.tensor_copy(out=cbf[:], in_=cbh[:]), reads=[('cbh',)], writes=[('cbf',)])
    P.add('dve', lambda h: h.tensor_tensor(out=cbl[:], in0=cbsrc, in1=cbf[:], op=ALU.subtract),
          reads=[('cbf',), ALLC], writes=[('cbl',)])
    bk, bkk = bank()

    def f(h, bk=bk):
        h.matmul(bk[:, 0:16], lhsT=j128_bf[:], rhs=cbh[:], start=True, stop=False)
        return h.matmul(bk[:, 0:16], lhsT=j128_bf[:], rhs=cbl[:], start=False, stop=True)
    P.add('pe', f, reads=[('j128',), ('cbh',), ('cbl',)], writes=[bkk])
    P.add('dve', lambda h, bk=bk: h.tensor_tensor(out=cbc[:], in0=cbsrc, in1=bk[:, 0:16], op=ALU.subtract),
          reads=[bkk, ALLC], writes=[('cbc',)])

    conv_ops = []
    converted = set()
    mode = {'cast': False}

    def emit_conversion(li, n):
        if (li, n) in converted or n.startswith('cm'):
            return
        converted.add((li, n))
        i = len(conv_ops)
        idx = li * NSB + SIDX[n]
        widx = li * NBL + BIDX[n]
        ncols = 2048 if n == 'ple' else 4096
        extra = [conv_ops[i - NCONV]] if i >= NCONV else []
        op = P.add('pool', lambda h: h.dma_start(out=wscr_d[idx, :, 0:ncols], in_=wts_d[widx, :, 0:ncols]),
                   writes=[('scr', idx)], sem=dsem('cv%d' % (i % NCONV)), extra=extra)
        conv_ops.append(op)

    def load_block(li, name):
        s = st['slot']
        st['slot'] = (s + 1) % NS
        idx = li * NSB + SIDX[name]
        ncols = 2048 if name == 'ple' else (CMW if name.startswith('cm') else 4096)
        if mode['cast'] and not name.startswith('cm'):
            widx = li * NBL + BIDX[name]
            P.add('pool', lambda h: h.dma_start(out=wring[:, s, 0:ncols], in_=wts_d[widx, :, 0:ncols]),
                  writes=[('w', s)], sem=dsem('wc%d' % s))
            if (li, name) not in converted:
                converted.add((li, name))
                P.add('sp', lambda h: h.dma_start(out=wscr_d[idx, :, 0:ncols], in_=wring[:, s, 0:ncols]),
                      reads=[('w', s)], writes=[('scr', idx)], sem=dsem('wb%d' % s))
            return s
        P.add('sp', lambda h: h.dma_start(out=wring[:, s, 0:ncols], in_=wscr_d[idx, :, 0:ncols]),
              reads=[('scr', idx)], writes=[('w', s)], sem=dsem('w%d' % s))
        return s

    def xkeys(xb):
        return [('x', xb, kc) for kc in range(8)]

    def norm(xb, nt, gbase, l, out, okeys, warm=0, fill=None):
        P.label = 'norm%d' % gbase
        P.add('act', lambda h: h.activation(out=dummy[:, 1:2], in_=dummy[:, 0:1], func=AF.Ln),
              reads=[('dummy', 0)], writes=[('dummy', 1)])
        bk, bkk = bank()
        for kc in range(8):
            P.add('act', lambda h, kc=kc: h.activation(out=By[:, kc, :nt], in_=xT[xb][:, kc, :nt], func=AF.Square),
                  reads=[('x', xb, kc)], writes=[('By', kc)])
            P.add('pe', lambda h, kc=kc: h.matmul(bk[:, :nt], lhsT=ones_bf[:], rhs=By[:, kc, :nt],
                                                  start=(kc == 0), stop=(kc == 7)),
                  reads=[('By', kc), ('ones',)], writes=[bkk])
        if fill:
            fill()
        if warm:
            wb, wbk = bank()

            def fw(h):
                for _ in range(warm):
                    ins = h.matmul(wb[:, :], lhsT=ones_bf[:], rhs=wsTbf[:, 0, 0:4, :].rearrange("p a b -> p (a b)"),
                                   start=True, stop=True)
                return ins
            P.add('pe', fw, reads=[('ones',), ('wsT',)], writes=[wbk])
        s, sk = frs()
        P.add('act', lambda h: h.activation(out=s[:, :nt], in_=bk[:, :nt], func=AF.Ln, bias=eps[:, 0:1],
                                            scale=1.0 / D), reads=[bkk, ('eps', 0)], writes=[sk])
        P.add('act', lambda h: h.activation(out=s[:, :nt], in_=s[:, :nt], func=AF.Exp, scale=-0.5),
              reads=[sk], writes=[sk])
        for kc in range(8):
            P.add('dve', lambda h, kc=kc: h.scalar_tensor_tensor(
                out=out[:, kc, :nt], in0=xT[xb][:, kc, :nt], scalar=ccol(gbase, l, kc), in1=s[:, :nt],
                op0=ALU.mult, op1=ALU.mult), reads=[('x', xb, kc), sk, ALLC], writes=[okeys[kc]])

    def wview(s, nk=8):
        if nk == 8:
            return wring[:, s, :].rearrange("p (k n) -> p k n", k=8)
        return wring[:, s, 0:2048].rearrange("p (k n) -> p k n", k=2)

    bgq = []

    def drain(n):
        for _ in range(n):
            if bgq:
                bgq.pop(0)()

    def proj(s, m, rhs, rkeys, nt, nk=8):
        bk, bkk = bank()
        wv = wview(s, nk)

        def f(h):
            for kc in range(nk):
                ins = h.matmul(bk[:, :nt], lhsT=wv[:, kc, m * 128:(m + 1) * 128], rhs=rhs[:, kc, :nt],
                               start=(kc == 0), stop=(kc == nk - 1))
            return ins
        P.add('pe', f, reads=[('w', s)] + rkeys, writes=[bkk])
        drain(2)
        return bk, bkk

    BhK = [('Bh', kc) for kc in range(8)]
    BuK = [('Bu', kc) for kc in range(8)]
    ByK = [('By', kc) for kc in range(8)]

    def glu_half(l, li, hh, nt):
        s = load_block(li, 'gt%d' % hh)
        sgs = []
        for m in range(4):
            c = hh * 4 + m
            bk, bkk = proj(s, m, Bh, BhK, nt)
            sg, sgk = frs()
            P.add('act', lambda h, bk=bk, sg=sg, c=c: h.activation(out=sg[:, :nt], in_=bk[:, :nt], func=AF.Sigmoid,
                                                                   bias=bcol(l, BCH['gt'] + c)),
                  reads=[bkk, ALLC], writes=[sgk])
            sgs.append((sg, sgk))
        return sgs

    def ab_half(l, li, hh, nt, sgs, hl=None):
        hl = l if hl is None else hl
        s = load_block(li, 'ab%d' % hh)
        for m in range(4):
            c = hh * 4 + m
            bk, bkk = proj(s, m, Bh, BhK, nt)
            sg, sgk = sgs[m]
            P.add('dve', lambda h, bk=bk, sg=sg, c=c: h.scalar_tensor_tensor(
                out=hglu[:, hl, c, HW:HW + nt], in0=bk[:, :nt], scalar=bcol(l, BCH['ab'] + c), in1=sg[:, :nt],
                op0=ALU.add, op1=ALU.mult), reads=[bkk, sgk, ALLC], writes=[('hg', hl, c)])

    def head_copy(l, prev_nt, use_flag):
        hk = [('hg', l, c) for c in range(8)]
        if use_flag:
            P.add('dve', lambda h: h.tensor_scalar(out=hglu[:, l, :, 0:HW], in0=hglu[:, l, :, prev_nt:prev_nt + HW],
                                                   scalar1=consts[:, C_FLAG:C_FLAG + 1], scalar2=None, op0=ALU.mult),
                  reads=hk + [ALLC], writes=hk)
        else:
            P.add('dve', lambda h: h.tensor_copy(out=hglu[:, l, :, 0:HW], in_=hglu[:, l, :, prev_nt:prev_nt + HW]),
                  reads=hk, writes=hk)

    def layer_halo(l, li, nt, xb):
        norm(xb, nt, C_NG, l, Bh, BhK)
        for hh in range(2):
            sgs = glu_half(l, li, hh, nt)
            ab_half(l, li, hh, nt, sgs)

    def conv_phase(l, li, nt, hl, jit=False):
        st1 = {}
        st2 = {}
        jit_slots = {}

        def jit_stage0(c):
            cs = st['slot']
            st['slot'] = (cs + 1) % NS
            jit_slots[c] = cs
            col0 = C_CW + (l * 8 + c) * CW
            P.add('dve', lambda h: h.tensor_tensor(
                out=wring[:, cs, 0:CMW].rearrange("p (k n) -> p k n", n=128),
                in0=cmat[:].unsqueeze(1).broadcast_to([128, CW, 128]),
                in1=consts[:, col0:col0 + CW].unsqueeze(2).broadcast_to([128, CW, 128]),
                op=ALU.mult), reads=[('cmat',), ALLC], writes=[('w', cs)])
            idx = li * NSB + SIDX['cm%d' % c]
            P.add('act', lambda h: h.dma_start(out=wscr_d[idx, :, 0:CMW], in_=wring[:, cs, 0:CMW]),
                  reads=[('w', cs)], writes=[('scr', idx)], sem=dsem('cmw%d' % cs))
        if jit and 0 not in VEC_CHUNKS:
            jit_stage0(0)
        for it in range(10):
            c = it
            if c < 8:
                bk1, bk1k = bank()
                if c in VEC_CHUNKS:
                    j = VEC_CHUNKS.index(c)
                    P.add('pe', lambda h, bk1=bk1, j=j: h.matmul(bk1[:, :nt], lhsT=cmatb[:], rhs=ubb[:, j, :nt],
                                                                 start=True, stop=True),
                          reads=[('Ub', j), ('cmatb',)], writes=[bk1k])
                elif jit:
                    if c + 1 < 8 and (c + 1) not in VEC_CHUNKS:
                        jit_stage0(c + 1)
                    cs = jit_slots[c]
                    mk = []
                else:
                    mk = []
                    cs = load_block(li, 'cm%d' % c)

                def f(h, cs=cs, bk1=bk1, c=c):
                    for k in range(CW):
                        ins = h.matmul(bk1[:, :nt], lhsT=wring[:, cs, k * 128:(k + 1) * 128],
                                       rhs=hglu[:, hl, c, 2 + k:2 + k + nt], start=(k == 0), stop=(k == CW - 1))
                    return ins
                if c not in VEC_CHUNKS:
                    P.add('pe', f, reads=[('w', cs), ('hg', hl, c)] + mk, writes=[bk1k])
                hs, hsk = brs()
                P.add('act', lambda h, bk1=bk1, hs=hs, c=c: h.activation(out=hs[:, :nt], in_=bk1[:, :nt], func=AF.Square,
                                                                         bias=cbc[:, l * 8 + c:l * 8 + c + 1]),
                      reads=[bk1k, ('cbc',)], writes=[hsk])
                st1[c] = (bk1, bk1k, hs, hsk)
            pc = it - 1
            if 0 <= pc < 8:
                pb1, pb1k, phs, phsk = st1.pop(pc)
                bk2, bk2k = bank()
                P.add('pe', lambda h, bk2=bk2, phs=phs: h.matmul(bk2[:, :nt], lhsT=j128_bf[:], rhs=phs[:, :nt],
                                                                 start=True, stop=True),
                      reads=[phsk, ('j128',)], writes=[bk2k])
                sd, sdk = frs()
                P.add('act', lambda h, bk2=bk2, sd=sd: h.activation(out=sd[:, :nt], in_=bk2[:, :nt], func=AF.Ln,
                                                                    bias=eps[:, 1:2]),
                      reads=[bk2k, ('eps', 1)], writes=[sdk])
                P.add('act', lambda h, sd=sd: h.activation(out=sd[:, :nt], in_=sd[:, :nt], func=AF.Exp, scale=-0.5),
                      reads=[sdk], writes=[sdk])
                st2[pc] = (pb1, pb1k, sd, sdk)
            qc = it - 2
            if 0 <= qc < 8:
                pb1, pb1k, sd, sdk = st2.pop(qc)
                tb, tbk = frs()
                P.add('dve', lambda h, pb1=pb1, sd=sd, tb=tb, qc=qc: h.scalar_tensor_tensor(
                    out=tb[:, :nt], in0=pb1[:, :nt], scalar=cbc[:, l * 8 + qc:l * 8 + qc + 1], in1=sd[:, :nt],
                    op0=ALU.add, op1=ALU.mult), reads=[pb1k, sdk, ('cbc',)], writes=[tbk])
                hn, hnk = brs()
                P.add('act', lambda h, tb=tb, hn=hn, qc=qc: h.activation(
                    out=hn[:, :nt], in_=tb[:, :nt], func=AF.Silu, bias=ccol(C_GNB, l, qc), scale=ccol(C_GNG, l, qc)),
                    reads=[tbk, ALLC], writes=[hnk])
                P.add('dve', lambda h, hn=hn, qc=qc: h.tensor_tensor(out=By[:, qc, :nt], in0=hn[:, :nt],
                                                                     in1=Bv[:, qc, :nt], op=ALU.mult),
                      reads=[hnk, ('Bv', qc)], writes=[('By', qc)])

    def proj4_kouter(s, rhs, rkey, nt):
        bks = [bank() for _ in range(4)]
        wv = wview(s)
        for kc in range(8):
            def f(h, kc=kc):
                for m in range(4):
                    ins = h.matmul(bks[m][0][:, :nt], lhsT=wv[:, kc, m * 128:(m + 1) * 128], rhs=rhs[:, kc, :nt],
                                   start=(kc == 0), stop=(kc == 7))
                return ins
            P.add('pe', f, reads=[('w', s), (rkey, kc)], writes=[b[1] for b in bks])
            drain(1)
        return bks

    def layer_full(l, li, nt, xb, hl=None, hooks=None, skip_norm=False, jit=False):
        hl = l if hl is None else hl
        hooks = list(hooks or [])

        def hook():
            if hooks:
                hooks.pop(0)()
        nb = nt // 128
        if not skip_norm:
            norm(xb, nt, C_NG, l, Bh, BhK, warm=(12 if nt == T else 0))
        P.label = 'u'
        for hh in range(2):
            s = load_block(li, 'u%d' % hh)
            pre4 = proj4_kouter(s, Bh, 'Bh', nt) if hh == 0 else None
            for m in range(4):
                c = hh * 4 + m
                bk, bkk = pre4[m] if pre4 else proj(s, m, Bh, BhK, nt)
                P.add('act', lambda h, bk=bk, c=c: h.activation(out=Bu[:, c, :nt], in_=bk[:, :nt],
                                                                func=AF.Gelu_apprx_tanh, bias=bcol(l, BCH['u'] + c)),
                      reads=[bkk, ALLC], writes=[('Bu', c)])
        hook()
        P.label = 'za'
        for hh in range(2):
            s = load_block(li, 'za%d' % hh)
            for m in range(4):
                c = hh * 4 + m
                bk, bkk = proj(s, m, Bh, BhK, nt)
                z, zk = brs()
                P.add('act', lambda h, bk=bk, z=z, c=c: h.activation(out=z[:, :nt], in_=bk[:, :nt], func=AF.Silu,
                                                                     bias=bcol(l, BCH['za'] + c)),
                      reads=[bkk, ALLC], writes=[zk])
                P.add('dve', lambda h, z=z, c=c: h.tensor_tensor(out=Bu[:, c, :nt], in0=Bu[:, c, :nt], in1=z[:, :nt],
                                                                 op=ALU.mult),
                      reads=[zk, ('Bu', c)], writes=[('Bu', c)])
        hook()
        P.label = 'v'
        gv = F16[:].rearrange("p (b h) t -> p b (h t)", h=2)
        vhat = Bv[:].rearrange("p (b h) t -> p b (h t)", h=2)
        for hh in range(2):
            s = load_block(li, 'v%d' % hh)
            wv = wview(s)
            for b in range(nb):
                bk, bkk = bank()

                def f(h, bk=bk, wv=wv, b=b, hh=hh):
                    for kc in range(8):
                        h.matmul(bk[:, :], lhsT=Bh[:, kc, b * 128:(b + 1) * 128], rhs=wv[:, kc, :],
                                 start=(kc == 0), stop=False)
                    o = l * 1024 + hh * 512
                    return h.matmul(bk[:, :], lhsT=Eb[:], rhs=bv2[:, o:o + 512], start=False, stop=True)
                P.add('pe', f, reads=[('w', s), ('Eb',), ('bv2',)] + BhK, writes=[bkk])
                P.add('act', lambda h, bk=bk, b=b, hh=hh: h.activation(out=gv[:, b, hh * 512:(hh + 1) * 512], in_=bk[:, :],
                                                                       func=AF.Gelu_apprx_tanh),
                      reads=[bkk], writes=[('F', 2 * b + hh)])
        for b in range(nb):
            fk = [('F', 2 * b), ('F', 2 * b + 1)]
            for a in range(2):
                P.add('dve', lambda h, b=b, a=a: h.bn_stats(out=bst[:, b, a, :], in_=gv[:, b, a * 512:(a + 1) * 512]),
                      reads=[fk[a]], writes=[('bst', b, a)])
            P.add('dve', lambda h, b=b: h.bn_aggr(out=bmv[:, b, :], in_=bst[:, b, :, :].rearrange("p a n -> p (a n)")),
                  reads=[('bst', b, 0), ('bst', b, 1)], writes=[('bmv', b)])
            P.add('act', lambda h, b=b: h.activation(out=bsd[:, b, 0:1], in_=bmv[:, b, 1:2], func=AF.Sqrt,
                                                     bias=eps[:, 1:2]),
                  reads=[('bmv', b), ('eps', 1)], writes=[('bsd', b)])
            P.add('dve', lambda h, b=b: h.reciprocal(out=bsd[:, b, 0:1], in_=bsd[:, b, 0:1]),
                  reads=[('bsd', b)], writes=[('bsd', b)])
            P.add('dve', lambda h, b=b: h.scalar_tensor_tensor(out=bsd[:, b, 1:2], in0=bmv[:, b, 0:1], scalar=-1.0,
                                                               in1=bsd[:, b, 0:1], op0=ALU.mult, op1=ALU.mult),
                  reads=[('bsd', b), ('bmv', b)], writes=[('bsd2', b)])
            P.add('act', lambda h, b=b: h.activation(out=vhat[:, b, :], in_=gv[:, b, :], func=AF.Identity,
                                                     bias=bsd[:, b, 1:2], scale=bsd[:, b, 0:1]),
                  reads=fk + [('bsd', b), ('bsd2', b)], writes=[('Bv', 2 * b), ('Bv', 2 * b + 1)])
        hook()
        P.label = 'gt0'
        sgs0 = glu_half(l, li, 0, nt)
        P.label = 'glu'
        ab_half(l, li, 0, nt, sgs0, hl)
        for j, c in enumerate(VEC_CHUNKS):
            for k in range(CW):
                col = C_CW + (l * 8 + c) * CW + k
                if k == 0:
                    bgq.append(lambda j=j, c=c, col=col: P.add('dve', lambda h: h.tensor_scalar(
                        out=ubuf[:, j, :nt], in0=hglu[:, hl, c, 2:2 + nt], scalar1=consts[:, col:col + 1], scalar2=None,
                        op0=ALU.mult), reads=[('hg', hl, c), ALLC], writes=[('U', j)]))
                else:
                    bgq.append(lambda j=j, c=c, col=col, k=k: P.add('dve', lambda h: h.scalar_tensor_tensor(
                        out=ubuf[:, j, :nt], in0=hglu[:, hl, c, 2 + k:2 + k + nt], scalar=consts[:, col:col + 1],
                        in1=ubuf[:, j, :nt], op0=ALU.mult, op1=ALU.add), reads=[('hg', hl, c), ALLC, ('U', j)],
                        writes=[('U', j)]))
            bgq.append(lambda j=j: P.add('act', lambda h: h.activation(out=ubb[:, j, :nt], in_=ubuf[:, j, :nt],
                                                                       func=AF.Copy),
                                         reads=[('U', j)], writes=[('Ub', j)]))
        hook()
        P.label = 'mix'
        for g in range(8):
            bk, bkk = bank()

            def f(h, bk=bk, g=g):
                for b in range(nb):
                    ins = h.matmul(bk[:, b * 128:(b + 1) * 128], lhsT=vhat[:, b, g * 128:(g + 1) * 128],
                                   rhs=wsTbf[:, l, g, :], start=True, stop=True)
                return ins
            P.add('pe', f, reads=[('Bv', i) for i in range(2 * nb)] + [('wsT',)], writes=[bkk])
            ta, tak = frs()
            P.add('dve', lambda h, bk=bk, ta=ta, g=g: h.scalar_tensor_tensor(
                out=ta[:, :nt].rearrange("p (b i) -> p b i", i=128),
                in0=bk[:, :nt].rearrange("p (b i) -> p b i", i=128),
                scalar=ccol(C_LNG, l, g),
                in1=Ct[:, l, g, :].unsqueeze(1).broadcast_to([128, nb, 128]),
                op0=ALU.mult, op1=ALU.add), reads=[bkk, ('Ct',), ALLC], writes=[tak])
            P.add('dve', lambda h, ta=ta, g=g: h.tensor_tensor(out=Bu[:, g, :nt], in0=ta[:, :nt], in1=Bu[:, g, :nt],
                                                               op=ALU.mult),
                  reads=[tak, ('Bu', g)], writes=[('Bu', g)])
        P.label = 'glu'
        sgs1 = glu_half(l, li, 1, nt)
        ab_half(l, li, 1, nt, sgs1, hl)
        P.label = 'zb'
        for hh in range(2):
            s = load_block(li, 'zb%d' % hh)
            for m in range(4):
                c = hh * 4 + m
                bk, bkk = proj(s, m, Bh, BhK, nt)
                P.add('act', lambda h, bk=bk, c=c: h.activation(out=Bv[:, c, :nt], in_=bk[:, :nt], func=AF.Silu,
                                                                bias=bcol(l, BCH['zb'] + c)),
                      reads=[bkk, ALLC], writes=[('Bv', c)])
        hook()
        P.label = 'gapa'
        for hh in range(2):
            s = load_block(li, 'ga%d' % hh)
            sig = []
            for m in range(4):
                c = hh * 4 + m
                bk, bkk = proj(s, m, Bh, BhK, nt)
                sa, sak = frs()
                P.add('act', lambda h, bk=bk, sa=sa, c=c: h.activation(out=sa[:, :nt], in_=bk[:, :nt], func=AF.Sigmoid,
                                                                       bias=bcol(l, BCH['ga'] + c)),
                      reads=[bkk, ALLC], writes=[sak])
                sig.append((sa, sak))
            s2 = load_block(li, 'pa%d' % hh)
            pre4 = proj4_kouter(s2, Bu, 'Bu', nt) if hh == 0 else None
            for m in range(4):
                c = hh * 4 + m
                bk, bkk = pre4[m] if pre4 else proj(s2, m, Bu, BuK, nt)
                sa, sak = sig[m]
                P.add('dve', lambda h, bk=bk, sa=sa, c=c: h.tensor_tensor(out=F16[:, c, :nt], in0=bk[:, :nt],
                                                                          in1=sa[:, :nt], op=ALU.mult),
                      reads=[bkk, sak], writes=[('F', c)])
        hook()
        P.label = 'conv'
        drain(10 ** 6)
        conv_phase(l, li, nt, hl, jit)
        hook()
        P.label = 'gbpb'
        sig = []
        for hh in range(2):
            s = load_block(li, 'gb%d' % hh)
            for m in range(4):
                c = hh * 4 + m
                bk, bkk = proj(s, m, Bh, BhK, nt)
                sa, sak = frs()
                P.add('act', lambda h, bk=bk, sa=sa, c=c: h.activation(out=sa[:, :nt], in_=bk[:, :nt], func=AF.Sigmoid,
                                                                       bias=bcol(l, BCH['gb'] + c)),
                      reads=[bkk, ALLC], writes=[sak])
                sig.append((sa, sak))
        for hh in range(2):
            s2 = load_block(li, 'pb%d' % hh)
            pre4 = proj4_kouter(s2, By, 'By', nt) if hh == 0 else None
            for m in range(4):
                c = hh * 4 + m
                bk, bkk = pre4[m] if pre4 else proj(s2, m, By, ByK, nt)
                sa, sak = sig[c]
                mb, mbk = frs()
                P.add('dve', lambda h, bk=bk, sa=sa, mb=mb: h.tensor_tensor(out=mb[:, :nt], in0=bk[:, :nt],
                                                                            in1=sa[:, :nt], op=ALU.mult),
                      reads=[bkk, sak], writes=[mbk])
                P.add('dve', lambda h, mb=mb, c=c: h.tensor_tensor(out=Bu[:, c, :nt], in0=F16[:, c, :nt],
                                                                   in1=mb[:, :nt], op=ALU.add),
                      reads=[mbk, ('F', c)], writes=[('Bu', c)])
        hook()
        P.label = 'out'
        for hh in range(2):
            s = load_block(li, 'out%d' % hh)
            pre4 = proj4_kouter(s, Bu, 'Bu', nt) if hh == 0 else None
            for m in range(4):
                c = hh * 4 + m
                bk, bkk = pre4[m] if pre4 else proj(s, m, Bu, BuK, nt)
                P.add('dve', lambda h, bk=bk, c=c: h.tensor_tensor(out=xT[xb][:, c, :nt], in0=xT[xb][:, c, :nt],
                                                                   in1=bk[:, :nt], op=ALU.add),
                      reads=[bkk, ('x', xb, c)], writes=[('x', xb, c)])
        P.label = 'ple0'
        P.label = 'ple'
        P.add('act', lambda h: h.activation(out=pbf[:, :, :nt], in_=pst[:, li, :, :nt], func=AF.Copy),
              reads=[('pst', li)], writes=[('pbf',)])
        sp_ = load_block(li, 'ple')
        ple_banks = []

        def ple_fill():
            for c in range(7):
                ple_banks.append(proj(sp_, c, pbf, [('pbf',)], nt, nk=2))
        norm(xb, nt, C_PLEG, l, Bh, BhK, warm=0, fill=ple_fill)
        P.label = 'ple'
        ple_banks.append(proj(sp_, 7, pbf, [('pbf',)], nt, nk=2))
        for c, (bk2, bk2k) in enumerate(ple_banks):
            P.add('act', lambda h, bk2=bk2, c=c: h.activation(out=F16[:, c, :nt], in_=bk2[:, :nt], func=AF.Copy),
                  reads=[bk2k], writes=[('F', c)])
        for hh in range(2):
            s = load_block(li, 'pg%d' % hh)
            pre4 = proj4_kouter(s, Bh, 'Bh', nt) if hh == 0 else None
            for m in range(4):
                c = hh * 4 + m
                bk, bkk = pre4[m] if pre4 else proj(s, m, Bh, BhK, nt)
                pg, pgk = frs()
                P.add('act', lambda h, bk=bk, pg=pg: h.activation(out=pg[:, :nt], in_=bk[:, :nt], func=AF.Sigmoid),
                      reads=[bkk], writes=[pgk])
                P.add('dve', lambda h, pg=pg, c=c: h.tensor_tensor(out=pg[:, :nt], in0=F16[:, c, :nt],
                                                                   in1=pg[:, :nt], op=ALU.mult),
                      reads=[pgk, ('F', c)], writes=[pgk])
                P.add('dve', lambda h, pg=pg, c=c: h.tensor_tensor(out=xT[xb][:, c, :nt], in0=xT[xb][:, c, :nt],
                                                                   in1=pg[:, :nt], op=ALU.add),
                      reads=[pgk, ('x', xb, c)], writes=[('x', xb, c)])

    tiles = [('pre', PRE, 0, 0)] + [(i, T, PRE + i * T, (i + 1) % 2) for i in range(ntile)]
    FK = [('F', i) for i in range(8)]
    prev_nt = None
    last_store = None

    def xload(nt, col0, xb):
        P.add('sp', lambda h: h.dma_start(out=xT[xb][:, :, :nt], in_=xT_d[:, :, col0:col0 + nt]),
              writes=xkeys(xb), sem=dsem('x%d' % xb))

    def pload(nt, col0, lis):
        for li_ in lis:
            P.add('sp', lambda h, li_=li_: h.dma_start(out=pst[:, li_, :, :nt], in_=pT_d[li_, :, :, col0:col0 + nt]),
                  writes=[('pst', li_)], sem=dsem('p%d' % li_))

    def finish(tid, xb, nt):
        t0 = tid * T
        if final:
            norm(xb, nt, C_FG, 0, F16, FK)
            return P.add('pool', lambda h: h.dma_start(out=out_d[:, :, t0:t0 + T], in_=F16[:]), reads=FK, sem=dsem('st'))
        return P.add('pool', lambda h: h.dma_start(out=out_d[:, :, t0:t0 + T], in_=xT[xb][:]), reads=xkeys(xb),
                     sem=dsem('st'))

    if NL == 2 and tiles and len(tiles) > 1:
        l0, l1 = layers
        xload(PRE, 0, 0)
        xload(T, PRE, 1)
        pload(T, PRE, [0, 1])
        mode['cast'] = True
        layer_halo(l0, 0, PRE, 0)
        head_copy(l0, PRE, True)
        layer_full(l0, 0, T, 1, jit=True)
        mode['cast'] = False
        pload(PRE, 0, [0])
        layer_full(l0, 0, PRE, 0, hl=l1)
        mode['cast'] = True
        layer_halo(l1, 1, PRE, 0)
        head_copy(l1, PRE, True)
        layer_full(l1, 1, T, 1, jit=True)
        mode['cast'] = False
        for n_ in LAYER_BLOCKS:
            emit_conversion(0, n_)
            emit_conversion(1, n_)
        tiles = tiles[2:]

        def start_next(k):
            if k < len(tiles):
                _tid, _nt, _col0, _xb = tiles[k]
                xload(_nt, _col0, _xb)
                pload(_nt, _col0, range(NL))
                norm(_xb, _nt, C_NG, l0, Bh, BhK)
        start_next(0)
        last_store = finish(0, 1, T)
        for ti, (tid, nt, col0, xb) in enumerate(tiles):
            head_copy(l0, T, False)
            layer_full(l0, 0, nt, xb, skip_norm=True)
            head_copy(l1, T, False)
            layer_full(l1, 1, nt, xb)
            start_next(ti + 1)
            last_store = finish(tid, xb, nt)
        tiles = []
    first_flag = True
    for ti, (tid, nt, col0, xb) in enumerate(tiles):
        xload(nt, col0, xb)
        pload(nt, col0, range(NL))
        for li, l in enumerate(layers):
            if tid != 'pre':
                head_copy(l, prev_nt, first_flag and ti == 1)
            if tid == 'pre' and li == NL - 1:
                layer_halo(l, li, nt, xb)
            else:
                layer_full(l, li, nt, xb)
        if tid != 'pre':
            last_store = finish(tid, xb, nt)
        prev_nt = nt
    if last_store is None:
        last_store = P.add('pool', lambda h: h.dma_start(out=out_d[:, :, 0:T], in_=F16[:]), reads=FK, sem=dsem('st'))
    P.add('pool', None, extra=[last_store] + conv_ops[-NCONV:])

    P.finalize()
    with nc.Block() as block:
        @block.tensor
        def _(h):
            P.emit('pe', h, engsems, dmasems)

        @block.scalar
        def _(h):
            P.emit('act', h, engsems, dmasems)

        @block.vector
        def _(h):
            P.emit('dve', h, engsems, dmasems)

        @block.gpsimd
        def _(h):
            P.emit('pool', h, engsems, dmasems)

        @block.sync
        def _(h):
            P.emit('sp', h, engsems, dmasems)
    es.close()
    nc._dbg_prog = P
    return nc


def _wblock(w, c0):
    return np.ascontiguousarray(w[:, c0:c0 + 512].reshape(8, 128, 512).transpose(1, 0, 2)).reshape(128, 4096)


def _layer_blocks(inp, l):
    out = np.zeros((NBL, 128, 4096), np.float32)
    for i, n in enumerate(LAYER_BLOCKS):
        if n == 'ple':
            w = inp['w_ple'][l]
            out[i, :, 0:2048] = np.ascontiguousarray(w.reshape(2, 128, 1024).transpose(1, 0, 2)).reshape(128, 2048)
            continue
        base, hh = n[:-1], int(n[-1])
        if base in WIN_OFF:
            out[i] = _wblock(inp['w_in'][l], WIN_OFF[base] + hh * 512)
        else:
            w = {'pa': inp['w_pa'], 'pb': inp['w_pb'], 'out': inp['w_out'], 'pg': inp['w_ple_gate']}[base][l]
            out[i] = _wblock(w, hh * 512)
    return out


def _colT(v):
    return np.ascontiguousarray(np.asarray(v, np.float32).reshape(-1, 128).T)


def _consts(inp, flag):
    c = np.zeros((128, NCOL), np.float32)
    for l in range(2):
        c[:, C_BIN + l * 64:C_BIN + (l + 1) * 64] = _colT(inp['b_in'][l])
        for base, key in ((C_NG, 'norm_g'), (C_LNG, 'a_ln_g'), (C_LNB, 'a_ln_b'), (C_CB, 'b_conv_b'),
                          (C_GNG, 'b_gn_g'), (C_GNB, 'b_gn_b'), (C_PLEG, 'ple_norm_g')):
            c[:, base + l * 8:base + (l + 1) * 8] = _colT(inp[key][l])
        cw = np.asarray(inp['b_conv_w'][l], np.float32)
        c[:, C_CW + l * 8 * CW:C_CW + (l + 1) * 8 * CW] = cw.reshape(CW, 8, 128).transpose(2, 1, 0).reshape(128, 8 * CW)
    c[:, C_FG:C_FG + 8] = _colT(inp['final_g'])
    c[:, C_FLAG] = flag
    return c


def _feature_major(a):
    nt = a.shape[0]
    return np.ascontiguousarray(a.reshape(nt, -1, 128).transpose(2, 1, 0))


_PROG_CACHE = {}


def _get_prog(layers, final, ntile):
    key = (tuple(layers), final, ntile)
    if key not in _PROG_CACHE:
        _PROG_CACHE[key] = build_program(list(layers), final, ntile)
    return _PROG_CACHE[key]


def _make_in_maps(layers, xs, inp):
    B, S, _ = xs.shape
    half = S // 2
    wts = np.concatenate([_layer_blocks(inp, l) for l in layers], axis=0)
    a_ws = np.asarray(inp['a_ws'], np.float32)
    wsT = np.ascontiguousarray(a_ws.transpose(3, 0, 1, 2)).reshape(128, 2 * 8 * 128)
    bsrep = np.ascontiguousarray(np.broadcast_to(np.asarray(inp['a_bs'], np.float32).reshape(1, -1), (128, 2 * 8 * 128)))
    bvrow = np.ascontiguousarray(np.asarray(inp['b_in'], np.float32)[:, 1024:2048].reshape(1, 2048))
    cmat = np.eye(128, dtype=np.float32) - np.float32(1.0 / 128.0)
    p = np.asarray(inp['p'], np.float32)
    in_maps = []
    for core in range(2 * B):
        b, hf = core // 2, core % 2
        s0 = hf * half
        xc = np.zeros((PRE + half, D), np.float32)
        xc[PRE:] = xs[b, s0:s0 + half]
        pc = np.zeros((len(layers), PRE + half, 256), np.float32)
        for li, l in enumerate(layers):
            pc[li, PRE:] = p[l, b, s0:s0 + half]
        if hf == 1:
            xc[:PRE] = xs[b, s0 - PRE:s0]
            for li, l in enumerate(layers):
                pc[li, :PRE] = p[l, b, s0 - PRE:s0]
        pT = np.stack([_feature_major(pc[li]) for li in range(len(layers))], axis=0)
        in_maps.append(dict(xT=_feature_major(xc), pT=pT, wts=wts, consts=_consts(inp, float(hf)),
                            wsT=wsT, bsrep=bsrep, bvrow=bvrow, cmat=cmat))
    return in_maps


def _gather(results, B, S):
    half = S // 2
    out = np.empty((B, S, D), np.float32)
    for core in range(2 * B):
        b, hf = core // 2, core % 2
        o = np.asarray(results[core]["outT"])
        out[b, hf * half:(hf + 1) * half] = o.transpose(2, 1, 0).reshape(half, D)
    return out


def _run(layers, final, xs, inp):
    B, S, _ = xs.shape
    nc = _get_prog(layers, final, S // 2 // T)
    in_maps = _make_in_maps(layers, xs, inp)
    res = run_bass_kernel_spmd(nc, in_maps, core_ids=list(range(2 * B)))
    return _gather(res.results, B, S)


def kernel(**inputs):
    inp = {k: np.asarray(v) for k, v in inputs.items()}
    x = np.asarray(inp['x'], np.float32)
    if FUSED:
        return _run([0, 1], True, x, inp)
    x1 = _run([0], False, x, inp)
    return _run([1], True, x1, inp)
```
